# Optimizing a Trainium2 kernel written in Bass

```python
import math
import jax, jax.numpy as jnp
from jax import lax
import numpy as np

D_MODEL = 1024
BATCH = 8
SEQ = 2048
DEPTH = 2
DEC_BATCH = 128
DEC_SEQ = 1
PAST_LEN = 16384
PAGE_SIZE = 128

A_WIDTH = D_MODEL * 3 // 8
A_DK = 64
A_DV = 64
A_HEADS = A_WIDTH // A_DK
B_WIDTH = D_MODEL * 3 // 8
B_DV = 64
B_HEADS = B_WIDTH // B_DV
B_DK = B_DV // 2
B_QK = B_HEADS * B_DK
GLA_RANK = 16
GLA_TAU = 16.0
C_WIDTH = D_MODEL - A_WIDTH - B_WIDTH
C_GROUP = 16
C_GROUPS = C_WIDTH // C_GROUP
C_STATE = 64
MIX_WIDTH = A_WIDTH + B_WIDTH + C_WIDTH
CHUNK = 64
NORM_EPS = 1e-6
IN_SIZES = (A_WIDTH, A_WIDTH, A_WIDTH, A_WIDTH, B_QK, B_QK, B_WIDTH, B_WIDTH, GLA_RANK, C_WIDTH, C_WIDTH)
IN_TOTAL = sum(IN_SIZES)

kernel_name = 'hymba_style_hgrn2_gla_s5_step'


def _split_points():
    pts, acc = [], 0
    for s in IN_SIZES[:-1]:
        acc += s
        pts.append(acc)
    return pts


def rmsnorm(x, g):
    xf = x.astype(jnp.float32)
    r = lax.rsqrt(jnp.mean(xf * xf, axis=-1, keepdims=True) + NORM_EPS)
    return (xf * r * g.astype(jnp.float32)).astype(x.dtype)


def head_rmsnorm(o, g):
    r = lax.rsqrt(jnp.mean(o * o, axis=-1, keepdims=True) + NORM_EPS)
    return o * r * g.astype(jnp.float32).reshape(o.shape[2], o.shape[3])


def gated_recurrence(q, k, v, log_a, s0):
    bsz, t, h, dk = q.shape
    c = min(CHUNK, t)
    n = -(-t // c)
    pad = n * c - t
    if pad:
        pw = ((0, 0), (0, pad), (0, 0), (0, 0))
        q, k, v, log_a = (jnp.pad(q, pw), jnp.pad(k, pw), jnp.pad(v, pw), jnp.pad(log_a, pw))

    def to_chunks(a):
        return a.reshape(bsz, n, c, h, a.shape[-1]).swapaxes(0, 1)

    causal = jnp.tril(jnp.ones((c, c), dtype=bool))

    def step(S, inp):
        qc, kc, vc, gc = inp
        b = jnp.cumsum(gc, axis=1)
        o_inter = jnp.einsum('bthk,bhkv->bthv', qc * jnp.exp(b), S)
        diff = b[:, :, None] - b[:, None]
        decay = jnp.exp(jnp.where(causal[None, :, :, None, None], diff, -jnp.inf))
        att = jnp.einsum('bthk,bshk,btshk->bths', qc, kc, decay)
        o_intra = jnp.einsum('bths,bshv->bthv', att, vc)
        b_last = b[:, -1]
        S_new = jnp.exp(b_last)[..., None] * S + jnp.einsum(
            'bshk,bshv->bhkv', kc * jnp.exp(b_last[:, None] - b), vc)
        return S_new, o_inter + o_intra

    s_fin, o = lax.scan(step, s0, (to_chunks(q), to_chunks(k), to_chunks(v), to_chunks(log_a)))
    o = o.swapaxes(0, 1).reshape(bsz, n * c, h, v.shape[-1])[:, :t]
    return o, s_fin


def _complex_affine_combine(e1, e2):
    a1r, a1i, b1r, b1i = e1
    a2r, a2i, b2r, b2i = e2
    ar = a2r * a1r - a2i * a1i
    ai = a2r * a1i + a2i * a1r
    br = a2r * b1r - a2i * b1i + b2r
    bi = a2r * b1i + a2i * b1r + b2i
    return ar, ai, br, bi


def s5_ssm(u, A_re, A_im, B_re, B_im, C_re, C_im, D, log_dt, x0_re, x0_im):
    f32 = jnp.float32
    dt = jnp.exp(log_dt.astype(f32))[:, None]
    lr, li = A_re.astype(f32), A_im.astype(f32)
    mag = jnp.exp(lr * dt)
    ab_re = mag * jnp.cos(li * dt)
    ab_im = mag * jnp.sin(li * dt)
    den = lr * lr + li * li
    nr = ab_re - 1.0
    co_re = (nr * lr + ab_im * li) / den
    co_im = (ab_im * lr - nr * li) / den
    Br, Bi = B_re.astype(f32), B_im.astype(f32)
    bb_re = co_re[..., None] * Br - co_im[..., None] * Bi
    bb_im = co_re[..., None] * Bi + co_im[..., None] * Br
    bu_re = jnp.einsum('gpc,btgc->btgp', bb_re, u)
    bu_im = jnp.einsum('gpc,btgc->btgp', bb_im, u)
    a_re = jnp.broadcast_to(ab_re, bu_re.shape)
    a_im = jnp.broadcast_to(ab_im, bu_re.shape)
    ac_re, ac_im, h_re, h_im = lax.associative_scan(
        _complex_affine_combine, (a_re, a_im, bu_re, bu_im), axis=1)
    x_re = h_re + ac_re * x0_re[:, None] - ac_im * x0_im[:, None]
    x_im = h_im + ac_re * x0_im[:, None] + ac_im * x0_re[:, None]
    y = (jnp.einsum('gcp,btgp->btgc', C_re.astype(f32), x_re)
         - jnp.einsum('gcp,btgp->btgc', C_im.astype(f32), x_im)
         + D.astype(f32).reshape(C_GROUPS, C_GROUP) * u)
    return y, x_re[:, -1], x_im[:, -1]


def mixer_layer(x, s_hg, s_gla, s_re, s_im, lb, norm_g, w_in, hg_norm_g, gla_w2, gla_b2,
                gla_norm_g, A_re, A_im, B_re, B_im, C_re, C_im, D, log_dt, w_glu, b_glu, w_out):
    f32 = jnp.float32
    bsz, t, _ = x.shape
    h = rmsnorm(x, norm_g)
    p = jnp.einsum('btd,de->bte', h, w_in).astype(f32)
    qa, fa, ia, za, qb, kb, vb, zb, rb, uc, zc = jnp.split(p, _split_points(), axis=-1)

    lbf = lb.astype(f32)
    log_f = jnp.logaddexp(jnp.log(lbf), jnp.log1p(-lbf) + jax.nn.log_sigmoid(fa))
    k_a = (1.0 - lbf) * jax.nn.sigmoid(-fa)
    q_a = jax.nn.silu(qa)
    shp_a = (bsz, t, A_HEADS, A_DK)
    o_a, s_hg_new = gated_recurrence(q_a.reshape(shp_a), k_a.reshape(shp_a),
                                     ia.reshape(bsz, t, A_HEADS, A_DV), log_f.reshape(shp_a),
                                     s_hg.astype(f32))
    o_a = head_rmsnorm(o_a, hg_norm_g).reshape(bsz, t, A_WIDTH) * jax.nn.silu(za)

    log_g = jax.nn.log_sigmoid(rb @ gla_w2.astype(f32) + gla_b2.astype(f32)) / GLA_TAU
    shp_b = (bsz, t, B_HEADS, B_DK)
    o_b, s_gla_new = gated_recurrence((qb * B_DK ** -0.5).reshape(shp_b), kb.reshape(shp_b),
                                      vb.reshape(bsz, t, B_HEADS, B_DV), log_g.reshape(shp_b),
                                      s_gla.astype(f32))
    o_b = head_rmsnorm(o_b, gla_norm_g).reshape(bsz, t, B_WIDTH) * jax.nn.silu(zb)

    y_c, s_re_new, s_im_new = s5_ssm(uc.reshape(bsz, t, C_GROUPS, C_GROUP), A_re, A_im, B_re, B_im,
                                     C_re, C_im, D, log_dt, s_re.astype(f32), s_im.astype(f32))
    y_c = jax.nn.gelu(y_c.reshape(bsz, t, C_WIDTH))
    y_c = y_c * jax.nn.sigmoid(y_c @ w_glu.astype(f32) + b_glu.astype(f32))
    o_c = y_c * jax.nn.silu(zc)

    mix = jnp.concatenate([o_a, o_b, o_c], axis=-1).astype(x.dtype)
    out = jnp.einsum('bte,ed->btd', mix, w_out)
    return x + out.astype(x.dtype), s_hg_new, s_gla_new, s_re_new, s_im_new


def setup_inputs(seed: int = 0) -> dict:
    key = jax.random.key(seed)
    ks = jax.random.split(key, 32)
    f32 = jnp.float32

    def nrm(k, shape, scale):
        return jax.random.normal(k, shape, f32) * scale

    log_dt = jax.random.uniform(ks[20], (DEPTH, C_GROUPS), f32,
                                minval=math.log(1e-3), maxval=math.log(1e-1))
    a_im = math.pi * jnp.arange(C_STATE, dtype=f32)
    return {
        'x_prompt': nrm(ks[0], (BATCH, SEQ, D_MODEL), 1.0),
        'x_sample': nrm(ks[1], (DEC_BATCH, DEC_SEQ, D_MODEL), 1.0),
        'state_hgrn': nrm(ks[2], (DEPTH, DEC_BATCH, A_HEADS, A_DK, A_DV), 0.5),
        'state_gla': nrm(ks[3], (DEPTH, DEC_BATCH, B_HEADS, B_DK, B_DV), 1.0),
        'state_s5_re': nrm(ks[4], (DEPTH, DEC_BATCH, C_GROUPS, C_STATE), 0.5),
        'state_s5_im': nrm(ks[5], (DEPTH, DEC_BATCH, C_GROUPS, C_STATE), 0.5),
        'norm_g': 1.0 + nrm(ks[6], (DEPTH, D_MODEL), 0.02),
        'w_in': nrm(ks[7], (DEPTH, D_MODEL, IN_TOTAL), D_MODEL ** -0.5),
        'hg_lb_logits': nrm(ks[8], (DEPTH, A_WIDTH), 0.5),
        'hg_norm_g': 1.0 + nrm(ks[9], (DEPTH, A_WIDTH), 0.02),
        'gla_w2': nrm(ks[10], (DEPTH, GLA_RANK, B_QK), GLA_RANK ** -0.5),
        'gla_b2': nrm(ks[11], (DEPTH, B_QK), 0.01),
        'gla_norm_g': 1.0 + nrm(ks[12], (DEPTH, B_WIDTH), 0.02),
        's5_A_re': -0.5 + nrm(ks[13], (DEPTH, C_GROUPS, C_STATE), 0.01),
        's5_A_im': a_im + nrm(ks[14], (DEPTH, C_GROUPS, C_STATE), 0.01),
        's5_B_re': nrm(ks[15], (DEPTH, C_GROUPS, C_STATE, C_GROUP), (2 * C_GROUP) ** -0.5),
        's5_B_im': nrm(ks[16], (DEPTH, C_GROUPS, C_STATE, C_GROUP), (2 * C_GROUP) ** -0.5),
        's5_C_re': nrm(ks[17], (DEPTH, C_GROUPS, C_GROUP, C_STATE), C_STATE ** -0.5),
        's5_C_im': nrm(ks[18], (DEPTH, C_GROUPS, C_GROUP, C_STATE), C_STATE ** -0.5),
        's5_D': nrm(ks[19], (DEPTH, C_WIDTH), 1.0),
        's5_log_dt': log_dt,
        's5_w_glu': nrm(ks[21], (DEPTH, C_WIDTH, C_WIDTH), C_WIDTH ** -0.5),
        's5_b_glu': nrm(ks[22], (DEPTH, C_WIDTH), 0.01),
        'w_out': nrm(ks[23], (DEPTH, MIX_WIDTH, D_MODEL), MIX_WIDTH ** -0.5),
        'final_norm_g': 1.0 + nrm(ks[24], (D_MODEL,), 0.02),
    }


def reference(x_prompt, x_sample, state_hgrn, state_gla, state_s5_re, state_s5_im, norm_g, w_in,
              hg_lb_logits, hg_norm_g, gla_w2, gla_b2, gla_norm_g, s5_A_re, s5_A_im, s5_B_re,
              s5_B_im, s5_C_re, s5_C_im, s5_D, s5_log_dt, s5_w_glu, s5_b_glu, w_out, final_norm_g):
    f32 = jnp.float32
    cum = jnp.cumsum(jax.nn.softmax(hg_lb_logits.astype(f32), axis=0), axis=0)
    lb_all = cum - cum[0:1]

    bp = x_prompt.shape[0]
    z_hg = jnp.zeros((bp, A_HEADS, A_DK, A_DV), f32)
    z_gla = jnp.zeros((bp, B_HEADS, B_DK, B_DV), f32)
    z_s5 = jnp.zeros((bp, C_GROUPS, C_STATE), f32)

    yp, ys = x_prompt, x_sample
    hg_p, gla_p, re_p, im_p = [], [], [], []
    hg_s, gla_s, re_s, im_s = [], [], [], []
    for l in range(DEPTH):
        w = (norm_g[l], w_in[l], hg_norm_g[l], gla_w2[l], gla_b2[l], gla_norm_g[l],
             s5_A_re[l], s5_A_im[l], s5_B_re[l], s5_B_im[l], s5_C_re[l], s5_C_im[l],
             s5_D[l], s5_log_dt[l], s5_w_glu[l], s5_b_glu[l], w_out[l])
        yp, a1, a2, a3, a4 = mixer_layer(yp, z_hg, z_gla, z_s5, z_s5, lb_all[l], *w)
        ys, b1, b2, b3, b4 = mixer_layer(ys, state_hgrn[l], state_gla[l], state_s5_re[l],
                                         state_s5_im[l], lb_all[l], *w)
        hg_p.append(a1); gla_p.append(a2); re_p.append(a3); im_p.append(a4)
        hg_s.append(b1); gla_s.append(b2); re_s.append(b3); im_s.append(b4)

    y_prompt = rmsnorm(yp, final_norm_g)
    y_sample = rmsnorm(ys, final_norm_g)
    hgrn_prompt = jnp.stack(hg_p, axis=0)
    gla_prompt = jnp.stack(gla_p, axis=0)
    s5re_prompt = jnp.stack(re_p, axis=0)
    s5im_prompt = jnp.stack(im_p, axis=0)
    hgrn_sample = jnp.stack(hg_s, axis=0)
    gla_sample = jnp.stack(gla_s, axis=0)
    s5re_sample = jnp.stack(re_s, axis=0)
    s5im_sample = jnp.stack(im_s, axis=0)
    return (y_prompt, y_sample, hgrn_prompt, gla_prompt, s5re_prompt, s5im_prompt,
            hgrn_sample, gla_sample, s5re_sample, s5im_sample)
```

```python
import math
import numpy as np
import ml_dtypes
from contextlib import ExitStack
import concourse.bass as bass
import concourse.mybir as mybir
from concourse.bass_utils import run_bass_kernel_spmd

F32 = mybir.dt.float32
BF16 = mybir.dt.bfloat16
I32 = mybir.dt.int32
AF = mybir.ActivationFunctionType
ALU = mybir.AluOpType
AX = mybir.AxisListType

NCORES = 8
D = 1024
INT = 3216
EPS = 1e-6
SAME_ENGINE_SYNC = True
DEBUG = False
NOSYNC_ENGS = ('pe',)
TW = [3, 1, 1, 2]
SCHED = True
SAMPLE_LAST = False
ONE_REC = True
LAST_FIRST = False
PRIO_ALPHA = 0.05
ENG_M4 = 'dve'
HILO = True
RMS1 = True
POOL_BF16_FAST = True
HN2 = True
FF1 = True
S5S_PRIV = True
SAMPLE_SPLIT = True
ENG_BD = 'dve'
ENG_S1 = 'pool'
ENG_S2 = 'pool'
ENG_S3 = 'pool'
ENG_KTS = 'dve'
ENG_XS = 'dve'
POW = False
CRITPRINT = False
TAGPRINT = None
TAGN = 400
ENG_KT = 'dve'
SWIN = 256
PE_GHZ = 2.4
FILL = True
FILL_NS = 130.0
FILL_N = 256
PE_MODE_PEN = 150.0
FILL_MAX = 48
VERBOSE = False
DBG_NAMES = []
DBG = {}

O_QA, O_FA, O_IA, O_ZA, O_QB, O_KB, O_VB, O_ZB, O_RB, O_UC, O_ZC = (
    0, 384, 768, 1152, 1536, 1728, 1920, 2304, 2688, 2704, 2960)

CF = {}
_off = 0
for _n, _w in [("identf", 128), ("ublk", 128), ("lblk", 128), ("ind", 2), ("jcol", 1), ("njcol", 1),
               ("maskB", 512), ("maskC", 512), ("selC", 128), ("dsel", 256)]:
    CF[_n] = (_off, _w)
    _off += _w
NCF = _off
CB = {"ident": (0, 128), "u128": (128, 128), "ones": (256, 128)}
NCB = 642


def make_consts():
    cf = np.zeros((128, NCF), np.float32)
    s = np.arange(128)
    same = (s[:, None] // 64) == (s[None, :] // 64)
    cf[:, CF["identf"][0]:CF["identf"][0] + 128] = np.eye(128)
    cf[:, CF["ublk"][0]:CF["ublk"][0] + 128] = ((s[:, None] <= s[None, :]) & same)
    cf[:, CF["lblk"][0]:CF["lblk"][0] + 128] = ((s[:, None] > s[None, :]) & same)
    cf[:, CF["ind"][0] + 0] = (s < 64)
    cf[:, CF["ind"][0] + 1] = (s >= 64)
    cf[:, CF["jcol"][0]] = s
    cf[:, CF["njcol"][0]] = -s
    mB = np.zeros((2, 64, 4, 8, 16), np.float32)
    for g2 in range(2):
        for qq in range(4):
            mB[g2, :, qq, 2 * qq + g2, :] = 1
    cf[:, CF["maskB"][0]:CF["maskB"][0] + 512] = mB.reshape(128, 512)
    mC = np.zeros((8, 16, 4, 2, 64), np.float32)
    for g2 in range(2):
        for qq in range(4):
            mC[2 * qq + g2, :, qq, g2, :] = 1
    cf[:, CF["maskC"][0]:CF["maskC"][0] + 512] = mC.reshape(128, 512)
    sC = np.zeros((8, 16, 4, 2, 16), np.float32)
    for g2 in range(2):
        for qq in range(4):
            for c in range(16):
                sC[2 * qq + g2, c, qq, g2, c] = 1
    cf[:, CF["selC"][0]:CF["selC"][0] + 128] = sC.reshape(128, 128)
    dS = np.zeros((128, 16, 16), np.float32)
    for b in range(16):
        dS[:, b, b] = 1
    cf[:, CF["dsel"][0]:CF["dsel"][0] + 256] = dS.reshape(128, 256)
    cb = np.zeros((128, NCB), np.float32)
    cb[:, 0:128] = np.eye(128)
    cb[:, 128:256] = (s[:, None] <= s[None, :])
    cb[:, 256:384] = 1.0
    cb[:, 384:512] = cf[:, CF["ublk"][0]:CF["ublk"][0] + 128]
    cb[:, 512:640] = cf[:, CF["lblk"][0]:CF["lblk"][0] + 128]
    cb[:, 640:642] = cf[:, CF["ind"][0]:CF["ind"][0] + 2]
    return cf, cb.astype(ml_dtypes.bfloat16)


class Buf:
    def __init__(self, t, name):
        self.t = t
        self.name = name
        self.lw = None
        self.rd = []

    def __getitem__(self, idx):
        return V(self.t[idx], self)


class V:
    def __init__(self, ap, buf):
        self.ap = ap
        self.buf = buf

    def re(self, pat, **kw):
        return V(self.ap.rearrange(pat, **kw), self.buf)

    def bl(self, n):
        sh = list(self.ap.shape)
        return V(self.ap.unsqueeze(len(sh)).broadcast_to(sh + [n]), self.buf)

    def bm(self, n):
        sh = list(self.ap.shape)
        return V(self.ap.unsqueeze(1).broadcast_to([sh[0], n] + sh[1:]), self.buf)

    def __getitem__(self, idx):
        return V(self.ap[idx], self.buf)


class Eng:
    def __init__(self, name, h, sem):
        self.name = name
        self.h = h
        self.sem = sem
        self.count = 0
        self.known = {}


class Ctx:
    def __init__(self, nc, es):
        self.nc = nc
        self.es = es
        self.engs = {}
        for name, h in [("pe", nc.tensor), ("act", nc.scalar), ("dve", nc.vector), ("pool", nc.gpsimd)]:
            self.engs[name] = Eng(name, h, es.enter_context(nc.semaphore("s_" + name)))
        self.dq = {}
        for name, h, n in [("sync", nc.sync, 12), ("gq", nc.gpsimd, 8)]:
            sems = [es.enter_context(nc.semaphore("d_%s%d" % (name, i))) for i in range(n)]
            self.dq[name] = dict(h=h, sems=sems, cnt=[0] * n, i=0, known={})
        self.engs["sync"] = Eng("sync", nc.sync, None)
        self.dq["sync"]["eng"] = self.engs["sync"]
        self.dq["gq"]["eng"] = self.engs["pool"]
        self.nbuf = 0
        self.bg = None
        self.recording = None
        self.filler = None
        self.nfill = 0
        self.tag = ""
        self.in_bg = False
        self.nops = 0

    def sb(self, name, shape, dt):
        t = self.es.enter_context(self.nc.sbuf_tensor(name, shape, dt))
        return Buf(t, name)

    def ps(self, name, shape, dt):
        t = self.es.enter_context(self.nc.psum_tensor(name, shape, dt))
        return Buf(t, name)

    def dram(self, name, shape, dt, kind):
        t = self.nc.dram_tensor(name, shape, dt, kind=kind)
        return Buf(t.ap(), name)

    def _waits(self, eng, reads, writes):
        deps = []
        for v in reads:
            b = v.buf
            if b.lw is not None:
                deps.append(b.lw)
        for v in writes:
            b = v.buf
            if b.lw is not None:
                deps.append(b.lw)
            deps.extend(b.rd)
        need = {}
        for (sem, val, owner) in deps:
            if owner is eng and (not SAME_ENGINE_SYNC or eng.name in NOSYNC_ENGS):
                continue
            k = id(sem)
            if eng.known.get(k, 0) >= val:
                continue
            if k not in need or need[k][1] < val:
                need[k] = (sem, val)
        for k, (sem, val) in need.items():
            eng.h.wait_ge(sem, val)
            eng.known[k] = val

    def op(self, ename, fn, reads, writes, cost=None, func=None):
        if self.recording is not None:
            if cost is None:
                n = 1
                for d in list(writes[0].ap.shape)[1:]:
                    n *= int(d)
                if ename == "pool":
                    cost = (200 + 1.1 * n) if (POOL_BF16_FAST and writes[0].ap.dtype == BF16) else (260 + 3.3 * n)
                else:
                    cost = 75 + 1.25 * n
            self.recording.append(dict(kind="op", eng=ename, fn=fn, reads=list(reads), writes=list(writes),
                                       cost=cost, func=func, tag=self.tag))
            return
        if self.bg is not None and not self.in_bg:
            self.nops += 1
            if self.nops % 3 == 0:
                self.in_bg = True
                next(self.bg, None)
                self.in_bg = False
        eng = self.engs[ename]
        self._waits(eng, reads, writes)
        ins = fn()
        eng.count += 1
        ins.then_inc(eng.sem, 1)
        ev = (eng.sem, eng.count, eng)
        for v in writes:
            v.buf.lw = ev
            v.buf.rd = []
        for v in reads:
            if v.buf.lw is not ev:
                v.buf.rd.append(ev)

    def dma(self, qname, out, in_, extra_reads=(), **kw):
        if self.recording is not None:
            n = 1
            for d in list(out.ap.shape):
                n *= int(d)
            self.recording.append(dict(kind="dma", eng=("sync" if qname == "sync" else "pool"), q=qname, out=out, in_=in_,
                                       kw=kw, reads=[in_] + list(extra_reads), writes=[out], cost=2500 + n * 4 / 100.0,
                                       func=None, tag=self.tag, extra=list(extra_reads)))
            return
        q = self.dq[qname]
        eng = q["eng"]
        slot = q["i"] % len(q["sems"])
        q["i"] += 1
        sem = q["sems"][slot]
        if q["cnt"][slot] > 0:
            k = id(sem)
            val = 16 * q["cnt"][slot]
            if eng.known.get(k, 0) < val:
                eng.h.wait_ge(sem, val)
                eng.known[k] = val
        self._waits(eng, [in_] + list(extra_reads), [out])
        q["h"].dma_start(out=out.ap, in_=in_.ap, **kw).then_inc(sem, 16)
        q["cnt"][slot] += 1
        ev = (sem, 16 * q["cnt"][slot], None)
        out.buf.lw = ev
        out.buf.rd = []
        in_.buf.rd.append(ev)
        for v in extra_reads:
            v.buf.rd.append(ev)

    def schedule_and_emit(self, window=96):
        ops = self.recording
        self.recording = None
        n = len(ops)
        preds = [set() for _ in range(n)]
        lw = {}
        rd = {}
        for i, o in enumerate(ops):
            for v in o["reads"]:
                b = id(v.buf)
                if b in lw:
                    preds[i].add(lw[b])
            for v in o["writes"]:
                b = id(v.buf)
                if b in lw:
                    preds[i].add(lw[b])
                for r in rd.get(b, ()):
                    preds[i].add(r)
            for v in o["writes"]:
                b = id(v.buf)
                lw[b] = i
                rd[b] = []
            for v in o["reads"]:
                b = id(v.buf)
                if lw.get(b) != i:
                    rd.setdefault(b, []).append(i)
            preds[i].discard(i)
        succs = [[] for _ in range(n)]
        indeg = [0] * n
        for i in range(n):
            indeg[i] = len(preds[i])
            for p in preds[i]:
                succs[p].append(i)
        tail = [0.0] * n
        for i in range(n - 1, -1, -1):
            m = 0.0
            for sx in succs[i]:
                if tail[sx] > m:
                    m = tail[sx]
            tail[i] = m + ops[i]["cost"] + (0.0 if ops[i]["eng"] == "pe" else 150.0)
        import heapq
        ready = [i for i in range(n) if indeg[i] == 0]
        heapq.heapify(ready)
        eng_free = {}
        fin = [0.0] * n
        act_set = [None]
        pe_mode = [None]
        order = []

        def tset(o):
            f = o["func"]
            if f is None or o["eng"] != "act" or isinstance(f, tuple):
                return None
            if f == AF.Ln:
                return "B"
            if f == AF.Tanh:
                return "A"
            if f == AF.Sin:
                return "C"
            return None
        while ready:
            cand = heapq.nsmallest(window, ready)
            best = None
            for i in cand:
                o = ops[i]
                e = o["eng"]
                st = eng_free.get(e, 0.0)
                for p in preds[i]:
                    if ops[p]["eng"] == e and ops[p]["kind"] == "op":
                        lat = 0.0 if e == "pe" else 100.0
                    else:
                        lat = 200.0
                    t = fin[p] + lat
                    if t > st:
                        st = t
                ts_ = tset(o)
                pen = 1300.0 if (ts_ is not None and act_set[0] is not None and ts_ != act_set[0]) else 0.0
                if e == "pe" and isinstance(o["func"], tuple) and pe_mode[0] is not None and o["func"] != pe_mode[0]:
                    pen = PE_MODE_PEN
                key = (st + pen - PRIO_ALPHA * tail[i], i)
                if best is None or key < best[0]:
                    best = (key, i, st, pen)
            _, i, st, pen = best
            ready.remove(i)
            heapq.heapify(ready)
            o = ops[i]
            e = o["eng"]
            if e == "pe" and self.filler is not None:
                gap = st - eng_free.get("pe", 0.0)
                nf = int(gap // FILL_NS)
                if nf >= 2:
                    order.append(("fill", min(nf, FILL_MAX)))
            if o["kind"] == "dma":
                eng_free[e] = st + 60.0
                fin[i] = st + o["cost"]
            else:
                fin[i] = st + pen + o["cost"]
                eng_free[e] = fin[i]
                ts_ = tset(o)
                if ts_ is not None:
                    act_set[0] = ts_
                if e == "pe" and isinstance(o["func"], tuple):
                    pe_mode[0] = o["func"]
            order.append(i)
            for sidx in succs[i]:
                indeg[sidx] -= 1
                if indeg[sidx] == 0:
                    heapq.heappush(ready, sidx)
        self.sim_time = max(fin) if n else 0.0
        self.crit = max(tail) if n else 0.0
        if TAGPRINT:
            rows = []
            for j, o in enumerate(ops):
                if o["tag"] == TAGPRINT:
                    rows.append((fin[j] - o["cost"], o["eng"], o["cost"], o["writes"][0].buf.name))
            rows.sort()
            for r in rows[:TAGN]:
                print("   %9.1f %5s %7.0f %s" % (r[0] / 1e3, r[1], r[2], r[3]))
        if CRITPRINT:
            i = max(range(n), key=lambda j: tail[j])
            path = []
            while True:
                path.append(i)
                if not succs[i]:
                    break
                i = max(succs[i], key=lambda j: tail[j])
            import collections
            cnt = collections.Counter()
            for j in path:
                o = ops[j]
                cnt[(o["eng"], o["writes"][0].buf.name)] += o["cost"] + 150
            print("critical path ops", len(path))
            for k, v in cnt.most_common(40):
                print("   ", k, round(v / 1e3, 1))
            mid = len(path) // 2
            for j in path[mid:mid + 70]:
                o = ops[j]
                print("      ", o["tag"], o["eng"], round(o["cost"]), o["writes"][0].buf.name, [r.buf.name for r in o["reads"]])
        tags = {}
        for i, o in enumerate(ops):
            t = o["tag"]
            st_ = fin[i] - o["cost"]
            if t not in tags:
                tags[t] = [st_, fin[i], 0]
            tags[t][0] = min(tags[t][0], st_)
            tags[t][1] = max(tags[t][1], fin[i])
            tags[t][2] += 1
        self.sim_tags = {k: (round(v[0] / 1e3, 1), round(v[1] / 1e3, 1), v[2]) for k, v in tags.items()}
        busy = {}
        for o in ops:
            if o["kind"] == "op":
                busy[o["eng"]] = busy.get(o["eng"], 0.0) + o["cost"]
        self.sim_busy = {k: round(v / 1e3, 1) for k, v in busy.items()}
        for i in order:
            if isinstance(i, tuple):
                for k in range(i[1]):
                    self.filler(k)
                self.nfill += i[1]
                continue
            o = ops[i]
            if o["kind"] == "dma":
                self.dma(o["q"], o["out"], o["in_"], extra_reads=o.get("extra", ()), **o["kw"])
            else:
                self.op(o["eng"], o["fn"], o["reads"], o["writes"])

    def finish(self):
        for q in self.dq.values():
            for sem, c in zip(q["sems"], q["cnt"]):
                if c > 0:
                    q["eng"].h.wait_ge(sem, 16 * c)
        s = self.engs["sync"]
        for n in ("pe", "act", "dve", "pool"):
            e = self.engs[n]
            if e.count > 0:
                s.h.wait_ge(e.sem, e.count)

    @staticmethod
    def _n(v):
        n = 1
        for d in list(v.ap.shape)[1:]:
            n *= int(d)
        return n

    def mm(self, out, lhsT, rhs, start=True, stop=True):
        c = 20 + max(64, self._n(out)) / PE_GHZ
        if lhsT.ap.dtype == F32:
            c *= 4.5
        def _r(x):
            return 32 if x <= 32 else (64 if x <= 64 else 128)
        mode = ("pe", _r(int(lhsT.ap.shape[0])), _r(self._n(lhsT)))
        self.op("pe", lambda: self.nc.tensor.matmul(out.ap, lhsT=lhsT.ap, rhs=rhs.ap, start=start, stop=stop),
                [lhsT, rhs], [out], cost=c, func=mode)

    def tr(self, out, in_, ident):
        def _r(x):
            return 32 if x <= 32 else (64 if x <= 64 else 128)
        mode = ("pe", _r(int(in_.ap.shape[0])), _r(self._n(in_)))
        self.op("pe", lambda: self.nc.tensor.transpose(out.ap, in_.ap, ident.ap), [in_, ident], [out], cost=110.0,
                func=mode)

    def act(self, out, in_, func, scale=1.0, bias=None, accum=None):
        rd = [in_]
        kw = {}
        if isinstance(scale, V):
            rd.append(scale)
            kw["scale"] = scale.ap
        else:
            kw["scale"] = float(scale)
        if bias is not None:
            if isinstance(bias, V):
                rd.append(bias)
                kw["bias"] = bias.ap
            else:
                kw["bias"] = float(bias)
        wr = [out]
        if accum is not None:
            kw["accum_out"] = accum.ap
            wr.append(accum)
        self.op("act", lambda: self.nc.scalar.activation(out=out.ap, in_=in_.ap, func=func, **kw), rd, wr,
                cost=250 + 1.0 * self._n(out), func=func)

    def _e(self, ename):
        return {"dve": self.nc.vector, "pool": self.nc.gpsimd}[ename]

    def tt(self, ename, out, a, b, op):
        cost = None
        if ename == "dve" and out.ap.dtype == BF16 and a.ap.dtype == BF16 and b.ap.dtype == BF16:
            cost = 75 + 0.65 * self._n(out)
        self.op(ename, lambda: self._e(ename).tensor_tensor(out=out.ap, in0=a.ap, in1=b.ap, op=op), [a, b], [out],
                cost=cost)

    def ts(self, ename, out, a, s1, op0, s2=None, op1=None):
        rd = [a]
        k1 = s1.ap if isinstance(s1, V) else float(s1)
        if isinstance(s1, V):
            rd.append(s1)
        kw = {}
        if op1 is not None:
            kw["op1"] = op1
            k2 = s2.ap if isinstance(s2, V) else float(s2)
            if isinstance(s2, V):
                rd.append(s2)
        else:
            k2 = None
        self.op(ename, lambda: self._e(ename).tensor_scalar(out=out.ap, in0=a.ap, scalar1=k1, scalar2=k2, op0=op0, **kw),
                rd, [out])

    def stt(self, out, a, s, b, op0, op1):
        rd = [a, b]
        ks = s.ap if isinstance(s, V) else float(s)
        if isinstance(s, V):
            rd.append(s)
        self.op("dve", lambda: self.nc.vector.scalar_tensor_tensor(out=out.ap, in0=a.ap, scalar=ks, in1=b.ap,
                                                                    op0=op0, op1=op1), rd, [out])

    def cp(self, ename, out, in_):
        if ename == "act":
            self.act(out, in_, AF.Copy)
        else:
            self.op(ename, lambda: self._e(ename).tensor_copy(out=out.ap, in_=in_.ap), [in_], [out])

    def red(self, out, in_, op=ALU.add):
        self.op("dve", lambda: self.nc.vector.tensor_reduce(out=out.ap, in_=in_.ap, axis=AX.X, op=op), [in_], [out])

    def memset(self, ename, out, val):
        self.op(ename, lambda: self._e(ename).memset(out.ap, float(val)), [], [out])


LN_HALF = math.log(0.5)
LN_QS = math.log(32 ** -0.5)
GC = math.sqrt(2.0 / math.pi)
TWO_PI = 2.0 * math.pi


def build(nt=16, nl=2):
    nc = bass.Bass("TRN2", target_bir_lowering=False)
    es = ExitStack()
    with es:
        cx = Ctx(nc, es)
        NTOK = nt * 128

        def din(name, shape, dt=F32):
            return cx.dram(name, shape, dt, "ExternalInput")

        def dout(name, shape):
            return cx.dram(name, shape, F32, "ExternalOutput")

        xp = din("xp", [NTOK, D])
        xsm = din("xsm", [16, D])
        st_hg = din("st_hg", [2, 96, 4096])
        st_gl = din("st_gl", [2, 96, 2048])
        st_re = din("st_re", [2, 16, 1024])
        st_im = din("st_im", [2, 16, 1024])
        norm_gT = din("norm_gT", [128, 16])
        w_in = din("w_in", [2, D, INT])
        lbl = din("lbl", [2, 384])
        hg_g = din("hg_g", [2, 384])
        w2 = din("w2", [2, 16, 192])
        b2 = din("b2", [2, 192])
        gla_g = din("gla_g", [2, 384])
        A_re = din("A_re", [2, 1024])
        A_im = din("A_im", [2, 1024])
        B_re = din("B_re", [2, 16, 64, 16])
        B_im = din("B_im", [2, 16, 64, 16])
        C_re = din("C_re", [2, 16, 16, 64])
        C_im = din("C_im", [2, 16, 16, 64])
        Dv = din("Dv", [2, 256])
        ldt = din("ldt", [2, 16])
        wglu = din("wglu", [2, 256, 256])
        bglu = din("bglu", [2, 256])
        w_out = din("w_out", [2, D, D])
        gfin = din("gfin", [D])
        cf_d = din("cf", [128, NCF])
        cb_d = din("cb", [128, NCB], BF16)

        yp = dout("yp", [NTOK, D])
        ysm = dout("ysm", [16, D])
        hgp = dout("hgp", [2, 6, 64, 64])
        glp = dout("glp", [2, 6, 32, 64])
        rep = dout("rep", [2, 1024])
        imp = dout("imp", [2, 1024])
        hgs = dout("hgs", [2, 96, 4096])
        gls = dout("gls", [2, 96, 2048])
        res_o = dout("res_o", [2, 16, 1024])
        ims_o = dout("ims_o", [2, 16, 1024])

        y0 = cx.dram("y0", [NTOK, D], F32, "Internal")
        scr_s = [[cx.dram("scr_s%d_%d" % (mi, j), [16, 384 if (mi == 0 or j == 3) else 192], F32, "Internal")
                  for j in range(4)] for mi in range(2)]
        scr_o = cx.dram("scr_o", [2, 96, 64], F32, "Internal")

        sb, ps = cx.sb, cx.ps

        dbgbuf = [None]

        def dbg(name, v):
            if not DEBUG or name in DBG_NAMES:
                return
            DBG_NAMES.append(name)
            sh = list(v.ap.shape)
            n = int(np.prod(sh[1:]))
            P_ = sh[0]
            if dbgbuf[0] is None:
                dbgbuf[0] = sb("dbgbuf", [128, 1024], F32)
            d = cx.dram("dbg_" + name, [P_, n], F32, "ExternalOutput")
            dst = dbgbuf[0][0:P_, 0:n]
            if len(sh) == 3:
                dst = dst.re("p (a b) -> p a b", a=sh[1])
            elif len(sh) == 4:
                dst = dst.re("p (a b c) -> p a b c", a=sh[1], b=sh[2])
            cx.cp("dve", dst, v)
            cx.dma("gq", V(d.t, d), dbgbuf[0][0:P_, 0:n])

        cf = sb("cf_sb", [128, 388], F32)
        cb = sb("cb_sb", [128, NCB], BF16)
        cx.dma("sync", cf[:, :], cf_d[:, 0:388])
        cx.dma("sync", cb[:, :], cb_d[:, :])

        def cfv(name, rows=slice(0, 128)):
            o, w = CF[name]
            return cf[rows, o:o + w]

        ident = cb[:, 0:128]
        u128 = cb[:, 128:256]
        ones_row = cb[0:1, 256:384]
        ublk_b = cb[:, 384:512]
        lblk_b = cb[:, 512:640]
        ind_b = cb[:, 640:642]
        identf = cfv("identf")
        ublk = cfv("ublk")
        lblk = cfv("lblk")
        ind = cfv("ind")

        win = sb("win", [128, 8, INT], BF16)
        wout = sb("wout", [128, 8, D], BF16)
        wink = [Buf(win.t[:, kc, :], "win%d" % kc) for kc in range(8)]
        woutk = [Buf(wout.t[:, kc, :], "wout%d" % kc) for kc in range(8)]
        SW = 804
        gT = sb("gT", [128, 16], F32)
        cx.dma("sync", gT[:, :], norm_gT[:, :])
        Ere = sb("Ere", [128, 1024], F32)
        Eim = sb("Eim", [128, 1024], F32)
        Fre = sb("Fre", [128, 8, 128], BF16)
        Fim = sb("Fim", [128, 8, 128], BF16)
        f127 = sb("f127", [128, 2, 8], F32)
        a1t = sb("a1t", [128, 2, 8], F32)
        A128 = sb("A128", [128, 2, 8], F32)
        coT = sb("coT", [128, 2, 8], F32)
        BD = sb("BD", [128, 2, 2, 512], BF16)
        Cblk = sb("Cblk", [128, 2, 8, 32], BF16)
        Ddiag = sb("Ddiag", [128, 256], BF16)
        wgl = sb("wgl", [128, 2, 256], BF16)
        bgl = sb("bgl", [1, 256], BF16)
        w2b = sb("w2b", [16, 192], BF16)
        b2b = sb("b2b", [1, 192], BF16)
        c0r = sb("c0r", [128, 384], F32)
        c1r = sb("c1r", [128, 384], F32)
        gAr = sb("gAr", [128, 384], BF16)
        gBr = sb("gBr", [128, 384], BF16)
        gfr = sb("gfr", [128, D], BF16)

        class NS:
            pass

        def f32(name, shape):
            return sb(name, shape, F32)

        def b16(name, shape):
            return sb(name, shape, BF16)

        xin = [f32("xin0", [128, D]), f32("xin1", [128, D]), f32("xin2", [128, D])]
        cx.dma("sync", xin[2][:, :], V(gfin.t.partition_broadcast(128), gfin))
        cx.cp("dve", gfr[:, :], xin[2][:, :])
        stage = xin
        nhalf = f32("nhalf", [128, 8])
        cx.memset("dve", nhalf[:, :], -0.5)
        xsr = f32("xsr", [16, D])
        xsb = b16("xsb", [128, D])
        hT = b16("hT", [128, 8, 128])
        THA = f32("THA", [128, 512])
        smallA = f32("smallA", [128, 8])
        smallO = f32("smallO", [128, 8])
        rbT = b16("rbT", [16, 128])
        mixb = [b16("mix0", [128, D]), b16("mix1", [128, D])]
        mixT = b16("mixT", [128, 8, 128])

        def mkset(i):
            a = NS()
            for n in ("Q2", "FF", "LOGF", "GA", "QKB", "GB"):
                setattr(a, n, f32("%s%d" % (n, i), [128, 384]))
            a.GZC = f32("GZC%d" % i, [128, 256])
            a.LG = f32("LG%d" % i, [128, 192])
            a.va = b16("va%d" % i, [128, 384])
            a.vb = b16("vb%d" % i, [128, 384])
            a.uT = b16("uT%d" % i, [128, 256])
            return a
        sets = [mkset(0), mkset(1)]

        def mkrec(nm, K):
            b = NS()
            HK = 6 * K
            b.E1, b.E2, b.E3 = (f32("%sE%d" % (nm, j), [128, HK]) for j in range(3))
            b.OSB = f32(nm + "OSB", [128, 384])
            b.TH = f32(nm + "TH", [128, 384])
            b.qt, b.kt, b.kh = (b16("%s%s" % (nm, j), [128, HK]) for j in ("qt", "kt", "kh"))
            b.qTs = b16(nm + "qTs", [K, 768])
            b.kTs = b16(nm + "kTs", [K, 768])
            b.attS = b16(nm + "attS", [128, 768])
            b.small = f32(nm + "small", [128, 32])
            b.Dd = f32(nm + "Dd", [K, 12])
            b.S = f32(nm + "S", [K, 384])
            b.Sbf = [b16(nm + "Sbf%d" % j, [K, 384]) for j in range(2)]
            b.p = [ps(nm + "p%d" % j, [128, 512], F32) for j in range(2)]
            return b
        HB = mkrec("H", 64)
        GBs = mkrec("G", 32)
        fill_bank = None
        if FILL:
            fill_bank = GBs.p[1]
            GBs.p = [GBs.p[0], GBs.p[0]]
        SB = NS()
        SB.t = [f32("St%d" % j, [128, 512]) for j in range(4)]
        SB.TH = f32("STH", [128, 256])
        SB.ysb = SB.TH
        SB.wre = b16("Swre", [128, 512])
        SB.wim = b16("Swim", [128, 512])
        SB.zcre, SB.zcim = (b16("S" + n, [128, 4, 128]) for n in ("zcre", "zcim"))
        SB.xre, SB.nxim = (f32("S" + n, [128, 4, 128]) for n in ("xre", "nxim"))
        SB.tb = [b16("Stb%d" % j, [128, 512]) for j in range(4)]
        SB.z127 = f32("Sz127", [128, 2, 4])
        SB.xl = f32("Sxl", [128, 2, 4])
        SB.xreb = b16("Sxreb", [128, 4, 128])
        SB.nximb = b16("Snximb", [128, 4, 128])
        SB.ycb = b16("Sycb", [128, 256])
        SB.ycT = b16("SycT", [128, 256])
        SB.cre = f32("Scre", [128, 8])
        SB.cim = f32("Scim", [128, 8])
        SB.small = f32("Ssmall", [128, 8])
        SB.p = [ps("Sp%d" % j, [128, 512], F32) for j in range(2)]
        pA = [ps("pA%d" % j, [128, 512], F32) for j in range(2)]

        def bfv(bank):
            return V(bank.t[:, :].bitcast(BF16), bank)

        def run(g):
            for _ in g:
                pass

        PW = [V(xin[0].t[:, 0:512], xin[0]), V(xin[0].t[:, 512:1024], xin[0]),
              V(xin[1].t[:, 0:512], xin[1]), V(xin[1].t[:, 512:1024], xin[1]),
              SB.t[0][:, :], SB.t[1][:, :], SB.t[2][:, :], SB.t[3][:, :],
              V(xin[2].t[:, 0:512], xin[2]), V(xin[2].t[:, 512:1024], xin[2]),
              SB.xre[:, :, :].re("p a b -> p (a b)"), SB.nxim[:, :, :].re("p a b -> p (a b)")]
        tmpi = V(SB.xreb.t[:, :, :].rearrange("p a b -> p (a b)").bitcast(I32)[:, 0:256], SB.xreb)

        def fview(buf):
            t = buf.t
            ap = t[:, :] if len(t.shape) == 2 else t[:, :, :].rearrange("p a b -> p (a b)")
            return V(ap.bitcast(F32), buf)
        stages = [fview(x) for x in (mixb[0], mixb[1], xsb, hT, mixT)]
        SW2 = 402

        def load_weights(l):
            for kc in range(8):
                cx.dma("gq", wink[kc][:, :], w_in[l, kc * 128:(kc + 1) * 128, :],
                       extra_reads=[b_[0:1, 0:1] for b_ in prep_state.get("bufs", [])] if kc == 0 else ())
                cx.act(wink[kc][:, :], wink[kc][:, :], AF.Copy, scale=gT[:, l * 8 + kc:l * 8 + kc + 1])
                yield
            for kc in range(8):
                cx.dma("gq", woutk[kc][:, :], w_out[l, kc * 128:(kc + 1) * 128, :])
                yield

        prep_state = {}

        def prep_layer(l):
            pm = SB.p[0]
            s0, s1 = sets
            L_lb0, L_lb1, L_hg, L_gla, L_w2, L_b2, L_bglu = s0.Q2, s0.FF, s0.LOGF, s0.GA, s0.QKB, s0.GB, s0.GZC
            L_wg = [s1.Q2, s1.FF]
            L_Dv, L_sC = s1.LOGF, s1.GA
            L_B = [s1.QKB, s1.GB]
            L_C = [HB.E1, HB.E2]
            mB, mC = fview(mixb[0]), fview(mixb[1])
            tlb, pad = HB.OSB, HB.TH
            cx.dma("sync", L_lb0[:, 0:384], V(lbl.t[0].partition_broadcast(128), lbl))
            cx.dma("sync", L_lb1[:, 0:384], V(lbl.t[1].partition_broadcast(128), lbl))
            cx.dma("sync", L_hg[:, 0:384], V(hg_g.t[l].partition_broadcast(128), hg_g))
            cx.dma("sync", L_gla[:, 0:384], V(gla_g.t[l].partition_broadcast(128), gla_g))
            cx.dma("sync", L_w2[0:16, 0:192], w2[l, :, :])
            cx.dma("sync", L_b2[0:1, 0:192], b2[l:l + 1, :])
            cx.dma("sync", L_bglu[0:1, 0:256], bglu[l:l + 1, :])
            for kc in range(2):
                cx.dma("sync", L_wg[kc][:, 0:256], wglu[l, kc * 128:(kc + 1) * 128, :])
            cx.dma("sync", L_Dv[:, 0:256], V(Dv.t[l].partition_broadcast(128), Dv))
            cx.dma("sync", mB[:, 0:512], cf_d[:, CF["maskB"][0]:CF["maskB"][0] + 512])
            cx.dma("sync", mC[:, 0:512], cf_d[:, CF["maskC"][0]:CF["maskC"][0] + 512])
            cx.dma("sync", L_sC[:, 0:128], cf_d[:, CF["selC"][0]:CF["selC"][0] + 128])
            for ri, Bsrc in enumerate((B_re, B_im)):
                cx.dma("sync", L_B[ri][:, 0:128].re("p (q c) -> p q c", q=8),
                       V(Bsrc.t[l].rearrange("(q g2) p c -> (g2 p) q c", g2=2), Bsrc))
            for ri, Csrc in enumerate((C_re, C_im)):
                cx.dma("sync", L_C[ri][:, 0:128].re("p (k e) -> p k e", k=2),
                       V(Csrc.t[l].rearrange("(k g) c e -> (g c) k e", k=2), Csrc))
            gtl = gen_table_loads(l)
            prep_state["bufs"] = gtl + [L_lb0, L_lb1, L_hg, L_gla, L_w2, L_b2, L_bglu, L_wg[0], L_wg[1], L_Dv, L_sC,
                                  L_B[0], L_B[1], L_C[0], L_C[1], mixb[0], mixb[1]]
            yield
            if l == 0:
                cx.memset("dve", c0r[:, :], 0.5)
                cx.memset("dve", c1r[:, :], 0.5)
            else:
                cx.tt("dve", tlb[:, 0:384], L_lb1[:, 0:384], L_lb0[:, 0:384], ALU.subtract)
                cx.act(tlb[:, 0:384], tlb[:, 0:384], AF.Tanh, scale=0.5)
                cx.ts("dve", c0r[:, :], tlb[:, 0:384], 0.25, ALU.mult, 0.75, ALU.add)
                cx.ts("dve", c1r[:, :], tlb[:, 0:384], -0.25, ALU.mult, 0.25, ALU.add)
            cx.ts("dve", gAr[:, :], L_hg[:, 0:384], 0.5, ALU.mult)
            cx.ts("dve", gBr[:, :], L_gla[:, 0:384], 0.5, ALU.mult)
            cx.cp("dve", w2b[:, :], L_w2[0:16, 0:192])
            cx.cp("dve", b2b[:, :], L_b2[0:1, 0:192])
            cx.cp("dve", bgl[:, :], L_bglu[0:1, 0:256])
            for kc in range(2):
                cx.cp("dve", wgl[:, kc, :], L_wg[kc][:, 0:256])
            for kc in range(2):
                cx.tt("dve", Ddiag[:, kc * 128:(kc + 1) * 128], identf, L_Dv[:, kc * 128:(kc + 1) * 128], ALU.mult)
            selb = SB.wim
            cx.cp("dve", selb[:, 0:128], L_sC[:, 0:128])
            padb = SB.wre
            for ri in range(2):
                Bn = L_B[ri]
                for q in range(8):
                    kc, qq = q // 4, q % 4
                    cx.tt(ENG_BD, pad[:, 0:128].re("p (g c) -> p g c", g=8),
                          mB[:, qq * 128:(qq + 1) * 128].re("p (g c) -> p g c", g=8),
                          Bn[:, q * 16:(q + 1) * 16].bm(8), ALU.mult)
                    cx.cp(ENG_BD, padb[:, 0:128], pad[:, 0:128])
                    cx.mm(pm[:, 0:128], padb[:, 0:128], ident)
                    cx.cp("act", BD[:, kc, ri, qq * 128:(qq + 1) * 128], pm[:, 0:128])
            for ri in range(2):
                Cn = L_C[ri]
                for q in range(8):
                    kc, qq = q // 4, q % 4
                    cx.tt(ENG_BD, pad[:, 0:128].re("p (g e) -> p g e", g=2),
                          mC[:, qq * 128:(qq + 1) * 128].re("p (g e) -> p g e", g=2),
                          Cn[:, kc * 64:(kc + 1) * 64].bm(2), ALU.mult)
                    cx.cp(ENG_BD, padb[:, 0:128], pad[:, 0:128])
                    cx.mm(pm[:, 0:32], padb[:, 0:128], selb[:, qq * 32:(qq + 1) * 32])
                    cx.act(Cblk[:, ri, q, :], pm[:, 0:32], AF.Copy, scale=(1.0 if ri == 0 else -1.0))
            gen_tables(l)

        def reduce_angle(ang, tmpf, n):
            cx.ts("dve", tmpf[:, 0:n], ang[:, 0:n], 1.0 / TWO_PI, ALU.mult)
            for h2 in range(n // 256):
                sl = slice(h2 * 256, (h2 + 1) * 256)
                cx.cp("dve", tmpi, tmpf[:, sl])
                cx.cp("dve", tmpf[:, sl], tmpi)
            cx.stt(ang[:, 0:n], tmpf[:, 0:n], -TWO_PI, ang[:, 0:n], ALU.mult, ALU.add)
            cx.ts("dve", ang[:, 0:n], ang[:, 0:n], math.pi, ALU.min, -math.pi, ALU.max)

        def sincos(ang, sn, cs, tmpf):
            cx.act(sn[:, :], ang[:, :], AF.Sin)
            cx.act(tmpf[:, :], ang[:, :], AF.Abs)
            cx.act(cs[:, :], tmpf[:, :], AF.Sin, scale=-1.0, bias=math.pi / 2)

        fm = f32("fmtmp", [128, 12, 8])
        tmpi8 = sb("tmpi8", [128, 8], I32)

        def gen_table_loads(l):
            lr, li, dtb = PW[0], PW[1], PW[2]
            cx.dma("sync", fm[:, 0, :], V(A_re.t[l].rearrange("(q r) -> r q", q=8), A_re), allow_slow_non_contiguous=True)
            cx.dma("sync", fm[:, 1, :], V(A_im.t[l].rearrange("(q r) -> r q", q=8), A_im), allow_slow_non_contiguous=True)
            for g2 in range(2):
                cx.dma("sync", fm[g2 * 64:(g2 + 1) * 64, 2, :],
                       V(ldt.t[l].rearrange("(q g) -> g q", g=2)[g2].partition_broadcast(64), ldt),
                       allow_slow_non_contiguous=True)
            cx.dma("sync", HB.small[:, 0:16], V(ldt.t[l].partition_broadcast(128), ldt))
            for hf in range(2):
                sl = slice(hf * 512, (hf + 1) * 512)
                lr_, li_ = (lr, li) if hf == 0 else (dtb, THA[:, :])
                cx.dma("sync", lr_[:, :], V(A_re.t[l, sl].partition_broadcast(128), A_re))
                cx.dma("sync", li_[:, :], V(A_im.t[l, sl].partition_broadcast(128), A_im))
            return [fm, HB.small, xin[0], xin[1], THA]

        def gen_tables(l):
            lr, li, dtb, lrdt, lidt, ang, tf, mg, cs, sn, dg, onesf = PW
            pm = SB.p[0]
            small = HB.small
            jcol = cfv("jcol")
            njcol = cfv("njcol")
            lrT, liT, dtT, lrdtT, angT, snT, csT, mgT, tA, tB, rec = [fm[:, k, :] for k in range(11)]
            cx.act(dtT, dtT, AF.Exp)
            cx.tt("dve", lrdtT, lrT, dtT, ALU.mult)
            cx.tt("dve", angT, liT, dtT, ALU.mult)
            cx.ts("dve", tA, angT, 1.0 / TWO_PI, ALU.mult)
            cx.cp("dve", tmpi8[:, :], tA)
            cx.cp("dve", tA, tmpi8[:, :])
            cx.stt(angT, tA, -TWO_PI, angT, ALU.mult, ALU.add)
            cx.ts("dve", angT, angT, math.pi, ALU.min, -math.pi, ALU.max)
            cx.act(snT, angT, AF.Sin)
            cx.act(tA, angT, AF.Abs)
            cx.act(csT, tA, AF.Sin, scale=-1.0, bias=math.pi / 2)
            cx.act(mgT, lrdtT, AF.Exp)
            cx.tt("dve", csT, csT, mgT, ALU.mult)
            cx.tt("dve", snT, snT, mgT, ALU.mult)
            cx.tt("dve", tA, lrT, lrT, ALU.mult)
            cx.tt("dve", tB, liT, liT, ALU.mult)
            cx.tt("dve", tA, tA, tB, ALU.add)
            cx.op("dve", lambda: nc.vector.reciprocal(out=rec.ap, in_=tA.ap), [tA], [rec])
            cx.ts("dve", csT, csT, -1.0, ALU.add)
            cx.tt("dve", tA, csT, lrT, ALU.mult)
            cx.tt("dve", tB, snT, liT, ALU.mult)
            cx.tt("dve", tA, tA, tB, ALU.add)
            cx.tt("dve", coT[:, 0, :], tA, rec, ALU.mult)
            cx.tt("dve", tA, snT, lrT, ALU.mult)
            cx.tt("dve", tB, csT, liT, ALU.mult)
            cx.tt("dve", tA, tA, tB, ALU.subtract)
            cx.tt("dve", coT[:, 1, :], tA, rec, ALU.mult)
            cx.memset("dve", onesf[:, 0:128], 1.0)
            cx.act(small[:, 16:32], small[:, 0:16], AF.Exp)
            for hf in range(2):
                sl = slice(hf * 512, (hf + 1) * 512)
                for ri in range(2):
                    for qq in range(4):
                        q = hf * 4 + qq
                        cx.ts("dve", dg[:, qq * 128:(qq + 1) * 128], identf, coT[:, ri, q:q + 1], ALU.mult)
                        cx.mm(pA[ri][:, qq * 128:(qq + 1) * 128], onesf[:, 0:128], dg[:, qq * 128:(qq + 1) * 128])
                core, coim = pA[0], pA[1]
                if hf == 1:
                    lr, li = dtb, THA[:, :]
                dt8 = small[:, 16 + hf * 8:16 + hf * 8 + 8].bl(64)
                cx.tt("dve", lrdt[:, :].re("p (g e) -> p g e", g=8), lr[:, :].re("p (g e) -> p g e", g=8), dt8, ALU.mult)
                cx.tt("dve", lidt[:, :].re("p (g e) -> p g e", g=8), li[:, :].re("p (g e) -> p g e", g=8), dt8, ALU.mult)
                cx.act(ang[:, :], lidt[:, :], AF.Copy, scale=jcol)
                reduce_angle(ang, tf, 512)
                sincos(ang, sn, cs, tf)
                cx.act(mg[:, :], lrdt[:, :], AF.Exp, scale=jcol)
                cx.tt("dve", tf[:, :], mg[:, :], cs[:, :], ALU.mult)
                for qq in range(4):
                    cx.tr(pm[:, qq * 128:(qq + 1) * 128], tf[:, qq * 128:(qq + 1) * 128], identf)
                cx.cp("act", Fre[:, hf * 4:hf * 4 + 4, :].re("p q j -> p (q j)"), pm[:, :])
                cx.cp("act", f127[:, 0, hf * 4:hf * 4 + 4], pm[:, :].re("p (q j) -> p q j", q=4)[:, :, 127])
                cx.cp("act", a1t[:, 0, hf * 4:hf * 4 + 4], pm[:, :].re("p (q j) -> p q j", q=4)[:, :, 1])
                cx.tt("dve", tf[:, :], mg[:, :], sn[:, :], ALU.mult)
                for qq in range(4):
                    cx.tr(pm[:, qq * 128:(qq + 1) * 128], tf[:, qq * 128:(qq + 1) * 128], identf)
                cx.cp("act", Fim[:, hf * 4:hf * 4 + 4, :].re("p q j -> p (q j)"), pm[:, :])
                cx.cp("act", f127[:, 1, hf * 4:hf * 4 + 4], pm[:, :].re("p (q j) -> p q j", q=4)[:, :, 127])
                cx.cp("act", a1t[:, 1, hf * 4:hf * 4 + 4], pm[:, :].re("p (q j) -> p q j", q=4)[:, :, 1])
                cx.act(mg[:, :], lrdt[:, :], AF.Exp, scale=njcol)
                cx.tt("dve", cs[:, :], cs[:, :], mg[:, :], ALU.mult)
                cx.tt("dve", sn[:, :], sn[:, :], mg[:, :], ALU.mult)
                cx.tt("dve", tf[:, :], core[:, :], cs[:, :], ALU.mult)
                cx.tt("dve", mg[:, :], coim[:, :], sn[:, :], ALU.mult)
                cx.tt("dve", Ere[:, sl], tf[:, :], mg[:, :], ALU.add)
                cx.tt("dve", tf[:, :], coim[:, :], cs[:, :], ALU.mult)
                cx.tt("dve", mg[:, :], core[:, :], sn[:, :], ALU.mult)
                cx.tt("dve", Eim[:, sl], tf[:, :], mg[:, :], ALU.subtract)
            sm8 = small[:, 0:8]
            cx.tt("dve", A128[:, 0, :], a1t[:, 0, :], f127[:, 0, :], ALU.mult)
            cx.tt("dve", sm8, a1t[:, 1, :], f127[:, 1, :], ALU.mult)
            cx.tt("dve", A128[:, 0, :], A128[:, 0, :], sm8, ALU.subtract)
            cx.tt("dve", A128[:, 1, :], a1t[:, 0, :], f127[:, 1, :], ALU.mult)
            cx.tt("dve", sm8, a1t[:, 1, :], f127[:, 0, :], ALU.mult)
            cx.tt("dve", A128[:, 1, :], A128[:, 1, :], sm8, ALU.add)

        def rms_stats(P, xsrc, junk, small):
            if RMS1:
                cx.act(junk[0:P, 0:1024], xsrc[0:P, 0:1024], AF.Square)
                cx.red(small[0:P, 2:3], junk[0:P, 0:1024])
            else:
                cx.act(junk[0:P, 0:512], xsrc[0:P, 0:512], AF.Square)
                cx.red(small[0:P, 0:1], junk[0:P, 0:512])
                cx.act(junk[0:P, 0:512], xsrc[0:P, 512:1024], AF.Square)
                cx.red(small[0:P, 1:2], junk[0:P, 0:512])
                cx.tt("dve", small[0:P, 2:3], small[0:P, 0:1], small[0:P, 1:2], ALU.add)
            cx.ts("dve", small[0:P, 2:3], small[0:P, 2:3], 1.0 / D, ALU.mult, EPS, ALU.add)
            if POW:
                cx.tt("pool", small[0:P, 4:5], small[0:P, 2:3], nhalf[0:P, 0:1], ALU.pow)
            else:
                cx.act(small[0:P, 3:4], small[0:P, 2:3], AF.Ln)
                cx.act(small[0:P, 4:5], small[0:P, 3:4], AF.Exp, scale=-0.5)

        def rms_scale(P, xsrc, out_bf):
            rms_stats(P, xsrc, out_bf if RMS1 else THA, smallA)
            if ENG_XS == "act":
                cx.act(out_bf[0:P, :], xsrc[0:P, :], AF.Copy, scale=smallA[0:P, 4:5])
            else:
                cx.ts("dve", out_bf[0:P, :], xsrc[0:P, :], smallA[0:P, 4:5], ALU.mult)

        def transposes_to(P, src, dstT, bank):
            pb = bfv(bank)
            for kc in range(8):
                cx.tr(pb[:, kc * 128:kc * 128 + P], src[0:P, kc * 128:(kc + 1) * 128], ident[0:P, 0:P])
            if P == 128:
                cx.cp("act", dstT[:, :, :].re("p k t -> p (k t)"), pb[:, :])
            else:
                cx.cp("act", dstT[:, :, 0:P], pb[:, :].re("p (k t) -> p k t", k=8)[:, :, 0:P])

        YG = 3

        def inproj(P, c0, n, pst):
            for kc in range(8):
                cx.mm(pst[0:P, 0:n], hT[:, kc, 0:P], wink[kc][:, c0:c0 + n], start=(kc == 0), stop=(kc == 7))
                if kc % YG == YG - 1:
                    yield

        def silu2(P, pst, n, dst):
            cx.act(THA[0:P, 0:n], pst[0:P, 0:n], AF.Tanh, scale=0.5)
            cx.stt(dst[0:P, 0:n], THA[0:P, 0:n], 1.0, pst[0:P, 0:n], ALU.add, ALU.mult)

        cur_layer = [0]

        def inproj_all(P, a):
            p0, p1 = pA
            yield from inproj(P, O_QA, 384, p0)
            silu2(P, p0, 384, a.Q2)
            yield
            yield from inproj(P, O_FA, 384, p1)
            cx.act(THA[0:P, 0:384], p1[0:P, 0:384], AF.Tanh, scale=0.5)
            if FF1 and cur_layer[0] == 0:
                cx.ts("dve", a.FF[0:P, 0:384], THA[0:P, 0:384], 0.5, ALU.mult, 0.5, ALU.add)
            else:
                cx.tt("dve", a.FF[0:P, 0:384], THA[0:P, 0:384], c1r[0:P, :], ALU.mult)
                cx.tt("dve", a.FF[0:P, 0:384], a.FF[0:P, 0:384], c0r[0:P, :], ALU.add)
            cx.act(a.LOGF[0:P, 0:384], a.FF[0:P, 0:384], AF.Ln)
            yield
            yield from inproj(P, O_IA, 384, p0)
            cx.cp("act", a.va[0:P, :], p0[0:P, 0:384])
            yield
            yield from inproj(P, O_ZA, 384, p1)
            silu2(P, p1, 384, a.GA)
            cx.tt("pool", a.GA[0:P, 0:384], a.GA[0:P, 0:384], gAr[0:P, :], ALU.mult)
            yield
            yield from inproj(P, O_QB, 384, p0)
            cx.cp("act", a.QKB[0:P, 0:384], p0[0:P, 0:384])
            yield
            yield from inproj(P, O_VB, 384, p1)
            cx.cp("act", a.vb[0:P, :], p1[0:P, 0:384])
            yield
            yield from inproj(P, O_ZB, 384, p0)
            silu2(P, p0, 384, a.GB)
            cx.tt("pool", a.GB[0:P, 0:384], a.GB[0:P, 0:384], gBr[0:P, :], ALU.mult)
            yield
            yield from inproj(P, O_ZC, 256, p1)
            silu2(P, p1, 256, a.GZC)
            yield
            for kc in range(8):
                cx.mm(p0[0:16, 0:P], wink[kc][:, O_RB:O_RB + 16], hT[:, kc, 0:P], start=(kc == 0), stop=(kc == 7))
            cx.cp("act", rbT[:, 0:P], p0[0:16, 0:P])
            for c in range(2):
                for kc in range(8):
                    cx.mm(p1[:, c * P:(c + 1) * P], wink[kc][:, O_UC + c * 128:O_UC + (c + 1) * 128], hT[:, kc, 0:P],
                          start=(kc == 0), stop=(kc == 7))
                    if kc % 4 == 3:
                        yield
            cx.cp("act", a.uT[:, 0:2 * P], p1[:, 0:2 * P])
            yield
            cx.mm(p0[0:P, 128:320], rbT[:, 0:P], w2b[:, :], start=True, stop=False)
            cx.mm(p0[0:P, 128:320], ones_row[:, 0:P], b2b[:, :], start=False, stop=True)
            cx.act(THA[0:P, 0:192], p0[0:P, 128:320], AF.Exp, scale=-1.0)
            cx.act(a.LG[0:P, :], THA[0:P, 0:192], AF.Ln, bias=1.0)
            yield

        def headnorm(P, pso, G, mixdst, b):
            if HN2:
                cx.act(b.TH[0:P, 0:384], pso[0:P, 0:384], AF.Square)
                o_, p_, g_ = b.OSB[0:P, 0:384], pso[0:P, 0:384], G[0:P, 0:384]
                cx.op("dve", lambda: nc.vector.tensor_tensor(out=o_.ap, in0=p_.ap, in1=g_.ap, op=ALU.mult),
                      [p_, g_, b.TH[0:P, 0:384]], [o_])
            else:
                cx.cp("act", b.OSB[0:P, 0:384], pso[0:P, 0:384])
                cx.act(b.TH[0:P, 0:384], b.OSB[0:P, 0:384], AF.Square)
            cx.red(b.small[0:P, 8:14], b.TH[0:P, 0:384].re("p (h v) -> p h v", h=6))
            cx.ts("dve", b.small[0:P, 8:14], b.small[0:P, 8:14], 1.0 / 64, ALU.mult, EPS, ALU.add)
            if POW:
                cx.tt("pool", b.small[0:P, 20:26], b.small[0:P, 8:14], nhalf[0:P, 0:6], ALU.pow)
            else:
                cx.act(b.small[0:P, 14:20], b.small[0:P, 8:14], AF.Ln)
                cx.act(b.small[0:P, 20:26], b.small[0:P, 14:20], AF.Exp, scale=-0.5)
            if HN2:
                cx.tt("dve", mixdst.re("p (h v) -> p h v", h=6), b.OSB[0:P, 0:384].re("p (h v) -> p h v", h=6),
                      b.small[0:P, 20:26].bl(64), ALU.mult)
            else:
                cx.tt("dve", b.OSB[0:P, 0:384].re("p (h v) -> p h v", h=6), b.OSB[0:P, 0:384].re("p (h v) -> p h v", h=6),
                      b.small[0:P, 20:26].bl(64), ALU.mult)
                cx.tt("dve", mixdst, b.OSB[0:P, 0:384], G[0:P, 0:384], ALU.mult)

        def recur(K, b, qsrc, ksrc_fn, lgsrc, escale, e1bias, vbf, G, mixdst, first, dlast, state_out):
            HK = 6 * K
            p0, p1 = b.p
            if first:
                cx.memset("dve", b.S[:, :], 0.0)
                cx.memset("dve", b.Sbf[0][:, :], 0.0)
            if HILO:
                hi, lo = b.kt, b.kh
                cx.cp("act", hi[:, 0:HK], lgsrc[:, 0:HK])
                cx.tt("dve", lo[:, 0:HK], lgsrc[:, 0:HK], hi[:, 0:HK], ALU.subtract)
                cx.mm(p0[:, 0:HK], ublk_b, hi[:, 0:HK], start=True, stop=False)
                cx.mm(p0[:, 0:HK], ublk_b, lo[:, 0:HK], start=False, stop=True)
                for h in range(6):
                    cx.mm(p0[0:K, 400 + 2 * h:402 + 2 * h], hi[:, h * K:(h + 1) * K], ind_b, start=True, stop=False)
                    cx.mm(p0[0:K, 400 + 2 * h:402 + 2 * h], lo[:, h * K:(h + 1) * K], ind_b, start=False, stop=True)
            else:
                cx.mm(p0[:, 0:HK], ublk, lgsrc[:, 0:HK])
                for h in range(6):
                    cx.mm(p0[0:K, 400 + 2 * h:402 + 2 * h], lgsrc[:, h * K:(h + 1) * K], ind)
                cx.mm(p1[:, 0:HK], lblk, lgsrc[:, 0:HK])
            yield
            cx.act(b.E1[:, 0:HK], p0[:, 0:HK], AF.Exp, scale=escale, bias=e1bias)
            cx.act(b.E2[:, 0:HK], p0[:, 0:HK], AF.Exp, scale=-escale)
            cx.act(b.Dd[0:K, :], p0[0:K, 400:412], AF.Exp, scale=escale)
            yield
            ksrc = ksrc_fn()
            cx.tt("dve", b.qt[:, 0:HK], qsrc[:, 0:HK], b.E1[:, 0:HK], ALU.mult)
            yield
            cx.tt(ENG_KT if K == 32 else "dve", b.kt[:, 0:HK], ksrc[:, 0:HK], b.E2[:, 0:HK], ALU.mult)
            yield
            pb0, pb1 = bfv(p0), bfv(p1)
            for h in range(6):
                cx.tr(pb0[0:K, h * 128:(h + 1) * 128], b.qt[:, h * K:(h + 1) * K], ident)
            cx.cp("act", b.qTs[0:K, :], pb0[0:K, 0:768])
            yield
            for h in range(6):
                cx.tr(pb1[0:K, h * 128:(h + 1) * 128], b.kt[:, h * K:(h + 1) * K], ident)
            cx.cp(ENG_KTS, b.kTs[0:K, :], pb1[0:K, 0:768])
            yield
            for h in range(4):
                cx.mm(p0[:, h * 128:(h + 1) * 128], b.kTs[0:K, h * 128:(h + 1) * 128], b.qTs[0:K, h * 128:(h + 1) * 128])
            yield
            cx.tt("dve", b.attS[:, 0:512].re("p (h t) -> p h t", h=4), p0[:, :].re("p (h t) -> p h t", h=4),
                  ublk.bm(4), ALU.mult)
            yield
            for h in range(4, 6):
                cx.mm(p1[:, (h - 4) * 128:(h - 3) * 128], b.kTs[0:K, h * 128:(h + 1) * 128],
                      b.qTs[0:K, h * 128:(h + 1) * 128])
            yield
            cx.tt("dve", b.attS[:, 512:768].re("p (h t) -> p h t", h=2), p1[:, 0:256].re("p (h t) -> p h t", h=2),
                  ublk.bm(2), ALU.mult)
            yield
            for c in range(2):
                pst = (p0, p1)[c]
                for h in range(6):
                    cx.mm(pst[0:K, h * 64:(h + 1) * 64], b.kt[c * 64:(c + 1) * 64, h * K:(h + 1) * K],
                          vbf[c * 64:(c + 1) * 64, h * 64:(h + 1) * 64])
                yield
                dsl = b.Dd[0:K, :].re("p (h c) -> p h c", c=2)[:, :, c].bl(64)
                cx.tt("dve", b.S[0:K, :], b.S[0:K, :], pst[0:K, 0:384], ALU.add)
                cx.tt("dve", b.S[0:K, :].re("p (h v) -> p h v", h=6), b.S[0:K, :].re("p (h v) -> p h v", h=6), dsl, ALU.mult)
                if c == 0:
                    cx.cp("act", b.Sbf[1][0:K, :], b.S[0:K, :])
                yield
            for h in range(6):
                osl = slice(h * 64, (h + 1) * 64)
                cx.mm(p0[:, osl], b.attS[:, h * 128:(h + 1) * 128], vbf[:, osl], start=True, stop=False)
                cx.mm(p0[0:64, osl], b.qTs[0:K, h * 128:h * 128 + 64], b.Sbf[0][0:K, osl], start=False, stop=True)
                cx.mm(p0[64:128, osl], b.qTs[0:K, h * 128 + 64:h * 128 + 128], b.Sbf[1][0:K, osl], start=False, stop=True)
            yield
            if dlast:
                cx.dma("sync", state_out, b.S[0:K, :].re("p (h v) -> p h v", h=6))
            else:
                cx.cp("act", b.Sbf[0][0:K, :], b.S[0:K, :])
            headnorm(128, p0, G, mixdst, b)
            yield

        def s5_y(P, hf, a, xr_fn, xi_fn):
            pst = SB.p[0]
            for qq in range(4):
                q = hf * 4 + qq
                ysl = slice(qq * 32, (qq + 1) * 32)
                cx.mm(pst[0:P, ysl], xr_fn(q), Cblk[:, 0, q, :], start=True, stop=False)
                cx.mm(pst[0:P, ysl], xi_fn(q), Cblk[:, 1, q, :], start=False, stop=False)
                cx.mm(pst[0:P, ysl], a.uT[:, hf * P:(hf + 1) * P], Ddiag[:, hf * 128 + qq * 32:hf * 128 + (qq + 1) * 32],
                      start=False, stop=True)
            cx.cp("act", SB.ysb[0:P, hf * 128:(hf + 1) * 128], pst[0:P, 0:128])

        def s5_tail(P, a, mixdst):
            y2, inn, th, ge2 = SB.t
            tg = SB.TH
            ysb = SB.ysb
            cx.act(y2[0:P, 0:256], ysb[0:P, :], AF.Square)
            cx.ts("dve", inn[0:P, 0:256], y2[0:P, 0:256], 0.044715, ALU.mult, 1.0, ALU.add)
            cx.tt("dve", inn[0:P, 0:256], inn[0:P, 0:256], ysb[0:P, :], ALU.mult)
            yield
            cx.act(th[0:P, 0:256], inn[0:P, 0:256], AF.Tanh, scale=GC)
            cx.stt(ge2[0:P, 0:256], th[0:P, 0:256], 1.0, ysb[0:P, :], ALU.add, ALU.mult)
            if ENG_S3 == "act":
                cx.act(SB.ycb[0:P, :], ge2[0:P, 0:256], AF.Copy, scale=0.5)
            else:
                cx.ts(ENG_S3, SB.ycb[0:P, :], ge2[0:P, 0:256], 0.5, ALU.mult, 1.0, ALU.mult)
            yield
            pb = bfv(SB.p[1])
            for kc in range(2):
                cx.tr(pb[:, kc * P:(kc + 1) * P], SB.ycb[0:P, kc * 128:(kc + 1) * 128], ident[0:P, 0:P])
            cx.cp("act", SB.ycT[:, 0:2 * P], pb[:, 0:2 * P])
            yield
            pg = SB.p[0]
            for kc in range(2):
                cx.mm(pg[0:P, 0:256], SB.ycT[:, kc * P:(kc + 1) * P], wgl[:, kc, :], start=(kc == 0), stop=False)
            cx.mm(pg[0:P, 0:256], ones_row[:, 0:P], bgl[:, :], start=False, stop=True)
            cx.act(tg[0:P, 0:256], pg[0:P, 0:256], AF.Tanh, scale=0.5)
            yield
            cx.stt(tg[0:P, 0:256], tg[0:P, 0:256], 1.0, ge2[0:P, 0:256], ALU.add, ALU.mult)
            cx.stt(mixdst, tg[0:P, 0:256], 0.125, a.GZC[0:P, 0:256], ALU.mult, ALU.mult)
            yield

        def s5_thread(l, a, mixdst, first, dlast):
            if first:
                cx.memset("dve", SB.cre[:, :], 0.0)
                cx.memset("dve", SB.cim[:, :], 0.0)
            t1, t2, t3, t4 = SB.tb
            f2 = "p q j -> p (q j)"
            for hf in range(2):
                sl = slice(hf * 512, (hf + 1) * 512)
                cx.mm(SB.p[0][:, :], a.uT[:, hf * 128:(hf + 1) * 128], BD[:, hf, 0, :])
                cx.mm(SB.p[1][:, :], a.uT[:, hf * 128:(hf + 1) * 128], BD[:, hf, 1, :])
                yield
                cx.tt("dve", t1[:, :], SB.p[0][:, :], Ere[:, sl], ALU.mult)
                yield
                cx.tt("dve", t2[:, :], SB.p[1][:, :], Eim[:, sl], ALU.mult)
                yield
                cx.tt("dve", t3[:, :], SB.p[0][:, :], Eim[:, sl], ALU.mult)
                yield
                cx.tt("dve", t4[:, :], SB.p[1][:, :], Ere[:, sl], ALU.mult)
                yield
                cx.tt("dve", SB.wre[:, :], t1[:, :], t2[:, :], ALU.subtract)
                yield
                cx.tt(ENG_S1, SB.wim[:, :], t3[:, :], t4[:, :], ALU.add)
                yield
                for k2 in range(2):
                    pst = SB.p[k2]
                    for ri, wsrc in enumerate((SB.wre, SB.wim)):
                        for qq in range(2):
                            ql = 2 * k2 + qq
                            cx.mm(pst[:, (ri * 2 + qq) * 128:(ri * 2 + qq + 1) * 128], wsrc[:, ql * 128:(ql + 1) * 128], u128)
                yield
                for k2 in range(2):
                    pst = SB.p[k2]
                    k = hf * 2 + k2
                    pre_ = pst[:, 0:256].re("p (q j) -> p q j", q=2)
                    pim_ = pst[:, 256:512].re("p (q j) -> p q j", q=2)
                    cx.tt("dve", SB.zcre[:, 2 * k2:2 * k2 + 2, :], pre_, SB.cre[:, 2 * k:2 * k + 2].bl(128), ALU.add)
                    cx.tt("dve", SB.zcim[:, 2 * k2:2 * k2 + 2, :], pim_, SB.cim[:, 2 * k:2 * k + 2].bl(128), ALU.add)
                    cx.tt("dve", SB.z127[:, 0, 2 * k2:2 * k2 + 2], pre_[:, :, 127], SB.cre[:, 2 * k:2 * k + 2], ALU.add)
                    cx.tt("dve", SB.z127[:, 1, 2 * k2:2 * k2 + 2], pim_[:, :, 127], SB.cim[:, 2 * k:2 * k + 2], ALU.add)
                    yield
                m1, m2, m3, m4 = SB.tb
                fsl = slice(hf * 4, hf * 4 + 4)
                cx.tt("dve", m1[:, :], Fre[:, fsl, :].re(f2), SB.zcre[:, :, :].re(f2), ALU.mult)
                yield
                cx.tt(ENG_S1, m2[:, :], Fim[:, fsl, :].re(f2), SB.zcim[:, :, :].re(f2), ALU.mult)
                yield
                cx.tt("dve", m3[:, :], Fre[:, fsl, :].re(f2), SB.zcim[:, :, :].re(f2), ALU.mult)
                yield
                cx.tt("dve", m4[:, :], Fim[:, fsl, :].re(f2), SB.zcre[:, :, :].re(f2), ALU.mult)
                yield
                cx.tt("dve", SB.xreb[:, :, :].re(f2), m1[:, :], m2[:, :], ALU.subtract)
                cx.tt("dve", SB.nximb[:, :, :].re(f2), m3[:, :], m4[:, :], ALU.add)
                yield
                zr, zi = SB.z127[:, 0, :], SB.z127[:, 1, :]
                sm = SB.small[:, 0:4]
                if dlast:
                    Fr, Fi = f127[:, 0, fsl], f127[:, 1, fsl]
                    Xr, Xi = SB.xl[:, 0, :], SB.xl[:, 1, :]
                    cx.tt("dve", Xr, Fr, zr, ALU.mult)
                    cx.tt("dve", sm, Fi, zi, ALU.mult)
                    cx.tt("dve", Xr, Xr, sm, ALU.subtract)
                    cx.tt("dve", Xi, Fr, zi, ALU.mult)
                    cx.tt("dve", sm, Fi, zr, ALU.mult)
                    cx.tt("dve", Xi, Xi, sm, ALU.add)
                    for g2 in range(2):
                        cx.dma("sync", V(rep.t[l, hf * 512:(hf + 1) * 512].rearrange("(q g p) -> g p q", q=4, g=2)[g2], rep),
                               SB.xl[g2 * 64:(g2 + 1) * 64, 0, :], allow_slow_non_contiguous=True)
                        cx.dma("sync", V(imp.t[l, hf * 512:(hf + 1) * 512].rearrange("(q g p) -> g p q", q=4, g=2)[g2], imp),
                               SB.xl[g2 * 64:(g2 + 1) * 64, 1, :], allow_slow_non_contiguous=True)
                else:
                    Ar, Ai = A128[:, 0, fsl], A128[:, 1, fsl]
                    cx.tt(ENG_S2, sm, Ar, zr, ALU.mult)
                    cx.tt(ENG_S2, SB.cre[:, fsl], Ai, zi, ALU.mult)
                    cx.tt(ENG_S2, SB.cre[:, fsl], sm, SB.cre[:, fsl], ALU.subtract)
                    cx.tt(ENG_S2, sm, Ar, zi, ALU.mult)
                    cx.tt(ENG_S2, SB.cim[:, fsl], Ai, zr, ALU.mult)
                    cx.tt(ENG_S2, SB.cim[:, fsl], SB.cim[:, fsl], sm, ALU.add)
                yield
                s5_y(128, hf, a, lambda q: SB.xreb[:, q % 4, :], lambda q: SB.nximb[:, q % 4, :])
                yield
            yield from s5_tail(128, a, mixdst)

        def outproj_residual(P, xres, m):
            transposes_to(P, m, mixT, pA[0])
            yield
            for n, pst in enumerate(pA):
                for kc in range(8):
                    cx.mm(pst[0:P, :], mixT[:, kc, 0:P], woutk[kc][:, n * 512:(n + 1) * 512], start=(kc == 0), stop=(kc == 7))
                    if kc % YG == YG - 1:
                        yield
                cx.tt("dve", xres[0:P, n * 512:(n + 1) * 512], xres[0:P, n * 512:(n + 1) * 512], pst[0:P, :], ALU.add)
                yield

        def final_norm(P, xres, dst, junk):
            rms_stats(P, xres, junk if RMS1 else THA, smallO)
            if RMS1:
                cx.stt(xres[0:P, :], xres[0:P, :], smallO[0:P, 4:5], gfr[0:P, :], ALU.mult, ALU.mult)
            else:
                for n in range(2):
                    sl = slice(n * 512, (n + 1) * 512)
                    cx.stt(xres[0:P, sl], xres[0:P, sl], smallO[0:P, 4:5], gfr[0:P, sl], ALU.mult, ALU.mult)
            cx.dma("sync", dst, xres[0:P, :])

        def thread_A(l, i):
            a = sets[i % 2]
            xb = xin[i % 3]
            src = xp if l == 0 else y0
            cx.dma("sync", xb[:, :], src[i * 128:(i + 1) * 128, :])
            rms_scale(128, xb, xsb)
            yield
            transposes_to(128, xsb, hT, pA[0])
            yield
            yield from inproj_all(128, a)

        def thread_O(l, i):
            xb = xin[i % 3]
            yield from outproj_residual(128, xb, mixb[i % 2])
            if l == nl - 1:
                final_norm(128, xb, yp[i * 128:(i + 1) * 128, :], mixb[i % 2])
            else:
                cx.dma("sync", y0[i * 128:(i + 1) * 128, :], xb[:, :])
            yield

        def thread_AO(l, g):
            if g >= 1:
                yield from thread_O(l, g - 1)
            if g + 1 < nt:
                yield from thread_A(l, g + 1)

        def thread_H(l, i):
            a = sets[i % 2]

            def ka_fn():
                cx.ts("pool", a.FF[:, 0:384], a.FF[:, 0:384], -1.0, ALU.mult, 1.0, ALU.add)
                return a.FF
            yield from recur(64, HB, a.Q2, ka_fn, a.LOGF, 1.0, LN_HALF, a.va, a.GA, mixb[i % 2][:, 0:384],
                             i == 0, i == nt - 1, V(hgp.t[l].rearrange("h k v -> k h v"), hgp))

        def thread_G(l, i):
            a = sets[i % 2]
            yield from recur(32, GBs, a.QKB, lambda: a.QKB[:, 192:384], a.LG, -1.0 / 16, LN_QS, a.vb, a.GB,
                             mixb[i % 2][:, 384:768], i == 0, i == nt - 1, V(glp.t[l].rearrange("h k v -> k h v"), glp))

        def thread_S(l, i):
            a = sets[i % 2]
            yield from s5_thread(l, a, mixb[i % 2][:, 768:1024], i == 0, i == nt - 1)

        def run_threads(gens, weights=None):
            active = list(gens)
            wts = dict(zip(map(id, gens), weights or [1] * len(gens)))
            while active:
                for g in list(active):
                    try:
                        for _ in range(wts[id(g)]):
                            next(g)
                    except StopIteration:
                        active.remove(g)

        class FV:
            def __init__(self, buf):
                self.v = fview(buf)

            def __getitem__(self, idx):
                return self.v[idx]
        xsb_f32, hT_f32, mixT_f32 = FV(xsb), FV(hT), FV(mixT)

        def sample_layer(l):
            P = 16
            a = sets[0]
            m = mixb[0]
            S0a, S1a = xin[1], xin[0]
            bh = V(SB.t[0].t[0:96, :].rearrange("p (a b) -> p a b", a=8), SB.t[0])
            obh = SB.t[1][0:96, 0:64]
            obh2 = SB.t[1][0:96, 64:128]
            if l == 0:
                cx.dma("sync", xsr[:, :], xsm[:, :])
            rms_scale(P, xsr, xsb)
            transposes_to(P, xsb, hT, pA[0])
            run(inproj_all(P, a))
            E1, E2, E3, OSB, TH = HB.E1, HB.E2, HB.E3, HB.OSB, HB.TH
            cx.ts("dve", E1[0:P, 0:384], a.FF[0:P, 0:384], -1.0, ALU.mult, 1.0, ALU.add)
            cx.ts("dve", E2[0:P, 0:384], a.Q2[0:P, 0:384], 0.5, ALU.mult)
            cx.cp("dve", E3[0:P, 0:384], a.va[0:P, :])
            cx.act(OSB[0:P, 0:192], a.LG[0:P, :], AF.Exp, scale=-1.0 / 16)
            cx.ts("dve", TH[0:P, 0:192], a.QKB[0:P, 0:192], 32 ** -0.5, ALU.mult)
            cx.cp("dve", a.LOGF[0:P, 0:384], a.vb[0:P, :])
            segs = [[a.FF[0:P, 0:384], E1[0:P, 0:384], E2[0:P, 0:384], E3[0:P, 0:384]],
                    [OSB[0:P, 0:192], a.QKB[0:P, 192:384], TH[0:P, 0:192], a.LOGF[0:P, 0:384]]]
            for mi in range(2):
                for j in range(4):
                    cx.dma("gq", scr_s[mi][j][:, :], segs[mi][j])
            for mi, (K, st_in, st_out, G, mixcols) in enumerate((
                    (64, st_hg, hgs, a.GA, slice(0, 384)),
                    (32, st_gl, gls, a.GB, slice(384, 768)))):
                if mi == 1 and SAMPLE_SPLIT:
                    bh = V(SB.t[2].t[0:96, :].rearrange("p (a b) -> p a b", a=8), SB.t[2])
                    obh = SB.t[3][0:96, 0:64]
                    obh2 = SB.t[3][0:96, 64:128]
                    flat = "p a b -> p (a b)"
                    s0bufs = [V(SB.xre.t[:, :, :].rearrange(flat), SB.xre), V(SB.nxim.t[:, :, :].rearrange(flat), SB.nxim)]
                    S1v = fview(mixb[1])
                else:
                    s0bufs = [xin[1][:, :], xin[2][:, :]]
                    S1v = xin[0][:, :]
                for j in range(3):
                    cx.dma("gq", bh[:, j, 0:K], V(scr_s[mi][j].t.rearrange("b (h k) -> (b h) k", h=6), scr_s[mi][j]))
                cx.dma("gq", bh[:, 3, :], V(scr_s[mi][3].t.rearrange("b (h v) -> (b h) v", h=6), scr_s[mi][3]))
                KH = K // 4
                KV = KH * 64
                s3 = "p (k v) -> p k v"
                for half in range(4):
                    ks = slice(half * KH, (half + 1) * KH)
                    a_, kk_, q_, v_ = bh[:, 0, ks], bh[:, 1, ks], bh[:, 2, ks], bh[:, 3, :]
                    S0 = s0bufs[half % 2][0:96, 0:KV]
                    S1 = S1v[0:96, 0:KV]
                    cx.dma("sync", S0, st_in[l, :, half * KV:(half + 1) * KV])
                    cx.tt("dve", S0.re(s3, k=KH), S0.re(s3, k=KH), a_.bl(64), ALU.mult)
                    cx.tt("dve", S1.re(s3, k=KH), kk_.bl(64), v_.bm(KH), ALU.mult)
                    cx.tt("dve", S0, S0, S1, ALU.add)
                    cx.dma("gq", st_out[l, :, half * KV:(half + 1) * KV], S0)
                    cx.tt("dve", S1.re(s3, k=KH), S0.re(s3, k=KH), q_.bl(64), ALU.mult)
                    if half == 0:
                        cx.red(obh, S1.re("p (k v) -> p v k", k=KH))
                    else:
                        cx.red(obh2, S1.re("p (k v) -> p v k", k=KH))
                        cx.tt("dve", obh, obh, obh2, ALU.add)
                cx.tt("dve", bh[:, 4, :], obh, obh, ALU.mult)
                cx.red(bh[:, 5, 0:1], bh[:, 4, :])
                cx.ts("dve", bh[:, 5, 0:1], bh[:, 5, 0:1], 1.0 / 64, ALU.mult, EPS, ALU.add)
                cx.act(bh[:, 5, 1:2], bh[:, 5, 0:1], AF.Ln)
                cx.act(bh[:, 5, 2:3], bh[:, 5, 1:2], AF.Exp, scale=-0.5)
                cx.ts("dve", obh, obh, bh[:, 5, 2:3], ALU.mult)
                cx.dma("gq", scr_o[mi, :, :], obh)
                otok = (E1, E2)[mi]
                cx.dma("gq", otok[0:P, 0:384], V(scr_o.t[mi].rearrange("(b h) v -> b (h v)", h=6), scr_o))
                cx.tt("dve", m[0:P, mixcols], otok[0:P, 0:384], G[0:P, 0:384], ALU.mult)
            s1_ = sets[1]
            if S5S_PRIV:
                x0T = V(s1_.Q2.t[:, 0:256].rearrange("p (r q b) -> p r q b", r=2, q=8), s1_.Q2)
                xsT = V(s1_.FF.t[:, 0:256].rearrange("p (r q b) -> p r q b", r=2, q=8), s1_.FF)
                x0w = [[THA, xsb_f32], [hT_f32, mixT_f32]]
            else:
                x0T = V(SB.xre.t[:, :, :].rearrange("p a b -> p (a b)")[:, 0:256].rearrange("p (r q b) -> p r q b", r=2, q=8), SB.xre)
                xsT = V(SB.xre.t[:, :, :].rearrange("p a b -> p (a b)")[:, 256:512].rearrange("p (r q b) -> p r q b", r=2, q=8), SB.xre)
                x0w = [[SB.t[0], SB.t[1]], [SB.t[2], SB.t[3]]]
            xsTb = V(SB.wre.t[:, 0:256].rearrange("p (r q b) -> p r q b", r=2, q=8), SB.wre)
            pq0, pq1 = SB.p
            for ri, stx in enumerate((st_re, st_im)):
                for hf in range(2):
                    cx.dma("sync", x0w[ri][hf][0:16, :], stx[l, :, hf * 512:(hf + 1) * 512])
            for ri in range(2):
                for q in range(8):
                    cx.tr(pq0[:, (ri * 8 + q) * 16:(ri * 8 + q + 1) * 16],
                          x0w[ri][q // 4][0:16, (q % 4) * 128:(q % 4 + 1) * 128], identf[0:16, 0:16])
            cx.cp("act", x0T.re("p r q b -> p (r q b)"), pq0[:, 0:256])
            for ri in range(2):
                for q in range(8):
                    kc, qq = q // 4, q % 4
                    cx.mm(pq1[:, (ri * 8 + q) * 16:(ri * 8 + q + 1) * 16], BD[:, kc, ri, qq * 128:(qq + 1) * 128],
                          a.uT[:, kc * P:(kc + 1) * P])
            Bu = pq1[:, 0:256].re("p (r q b) -> p r q b", r=2, q=8)
            if S5S_PRIV:
                tq = [V(buf.t[:, 0:128].rearrange("p (q b) -> p q b", q=8), buf)
                      for buf in (s1_.LOGF, s1_.GA, s1_.QKB, s1_.GB)]
            else:
                tq = [V(SB.nxim.t[:, :, :].rearrange("p a b -> p (a b)")[:, o_:o_ + 128].rearrange("p (q b) -> p q b", q=8), SB.nxim)
                      for o_ in (0, 128)] + \
                     [V(buf.t[:, 0:128].rearrange("p (q b) -> p q b", q=8), buf) for buf in (HB.E1, HB.E2)]
            cor, coi = coT[:, 0, :].bl(16), coT[:, 1, :].bl(16)
            ar, ai = a1t[:, 0, :].bl(16), a1t[:, 1, :].bl(16)
            cx.tt("dve", tq[0], Bu[:, 0], cor, ALU.mult)
            cx.tt("dve", tq[1], Bu[:, 1], coi, ALU.mult)
            cx.tt("dve", tq[0], tq[0], tq[1], ALU.subtract)
            cx.tt("dve", tq[1], Bu[:, 0], coi, ALU.mult)
            cx.tt("dve", tq[2], Bu[:, 1], cor, ALU.mult)
            cx.tt("dve", tq[1], tq[1], tq[2], ALU.add)
            cx.tt("dve", tq[2], x0T[:, 0], ar, ALU.mult)
            cx.tt("dve", tq[3], x0T[:, 1], ai, ALU.mult)
            cx.tt("dve", tq[2], tq[2], tq[3], ALU.subtract)
            cx.tt("dve", xsT[:, 0], tq[2], tq[0], ALU.add)
            cx.tt("dve", tq[2], x0T[:, 1], ar, ALU.mult)
            cx.tt("dve", tq[3], x0T[:, 0], ai, ALU.mult)
            cx.tt("dve", tq[2], tq[2], tq[3], ALU.add)
            cx.tt("dve", xsT[:, 1], tq[2], tq[1], ALU.add)
            cx.cp("dve", xsTb[:, 0], xsT[:, 0])
            cx.cp("dve", xsTb[:, 1], xsT[:, 1])
            for ri, dst in enumerate((res_o, ims_o)):
                for hf in range(2):
                    for qq in range(4):
                        cx.tr(SB.p[ri][0:16, qq * 128:(qq + 1) * 128], xsT[:, ri, hf * 4 + qq, :], identf)
                    cx.cp("act", x0w[ri][hf][0:16, :], SB.p[ri][0:16, :])
                    cx.dma("gq", dst[l, :, hf * 512:(hf + 1) * 512], x0w[ri][hf][0:16, :])
            for hf in range(2):
                s5_y(P, hf, a, lambda q: xsTb[:, 0, q, :], lambda q: xsTb[:, 1, q, :])
            run(s5_tail(P, a, m[0:P, 768:1024]))
            run(outproj_residual(P, xsr, m))
            if l == nl - 1:
                final_norm(P, xsr, ysm[:, :], m)

        if FILL:
            fdst = [fill_bank.t[:, 0:FILL_N], fill_bank.t[:, 512 - FILL_N:512]]
            fid = cb.t[:, 0:128]
            frhs = cb.t[:, 0:FILL_N]

            def filler(k):
                nc.tensor.matmul(fdst[k % 2], lhsT=fid, rhs=frhs, start=True, stop=True)
            cx.filler = filler
            cx.mm(fill_bank[:, 0:128], ident, ident)
        for l in range(nl):
            cur_layer[0] = l
            if SCHED:
                if cx.recording is None:
                    cx.recording = []
                cx.tag = "prep"
                pg = prep_layer(l)
                next(pg)
                cx.tag = "load"
                run(load_weights(l))
                cx.tag = "prep"
                run(pg)
            else:
                run(load_weights(l))
                run(prep_layer(l))
            if not (SAMPLE_LAST and (l < nl - 1 or not LAST_FIRST)):
                cx.tag = "sample"
                sample_layer(l)
            cx.tag = "tiles"
            run(thread_A(l, 0))
            for g in range(nt):
                cx.tag = "g%02d" % g
                run_threads([thread_S(l, g), thread_H(l, g), thread_G(l, g), thread_AO(l, g)], TW)
            run(thread_O(l, nt - 1))
            if SAMPLE_LAST and (l < nl - 1 or not LAST_FIRST):
                cx.tag = "sample"
                sample_layer(l)
            if SCHED and (l == nl - 1 or not ONE_REC):
                cx.schedule_and_emit(SWIN)
                print("layer", l, "sim_time_us", round(cx.sim_time / 1e3, 1), cx.sim_busy, "crit_us", round(cx.crit / 1e3, 1))
                if VERBOSE:
                    print(cx.sim_tags)
        cx.finish()
    return nc


_NC_CACHE = {}


def kernel(**inp):
    return kernel_impl(inp, 16, 2)


def kernel_impl(inp, nt, nl):
    f32 = np.float32
    g = lambda k: np.ascontiguousarray(np.asarray(inp[k], dtype=f32))
    cf, cb = make_consts()
    if (nt, nl) not in _NC_CACHE:
        _NC_CACHE[(nt, nl)] = build(nt, nl)
    nc = _NC_CACHE[(nt, nl)]
    x_prompt, x_sample = g("x_prompt"), g("x_sample")
    shg, sgl, sre, sim = g("state_hgrn"), g("state_gla"), g("state_s5_re"), g("state_s5_im")
    norm_gT = np.ascontiguousarray(g("norm_g").reshape(2, 8, 128).transpose(2, 0, 1).reshape(128, 16))
    shared = {
        "norm_gT": norm_gT, "w_in": g("w_in"), "lbl": g("hg_lb_logits"), "hg_g": g("hg_norm_g"),
        "w2": g("gla_w2"), "b2": g("gla_b2"), "gla_g": g("gla_norm_g"),
        "A_re": g("s5_A_re").reshape(2, 1024), "A_im": g("s5_A_im").reshape(2, 1024),
        "B_re": g("s5_B_re"), "B_im": g("s5_B_im"), "C_re": g("s5_C_re"), "C_im": g("s5_C_im"),
        "Dv": g("s5_D"), "ldt": g("s5_log_dt"), "wglu": g("s5_w_glu"), "bglu": g("s5_b_glu"),
        "w_out": g("w_out"), "gfin": g("final_norm_g"), "cf": cf, "cb": cb,
    }
    in_maps = []
    for c in range(NCORES):
        m = dict(shared)
        sl = slice(16 * c, 16 * (c + 1))
        m["xp"] = np.ascontiguousarray(x_prompt[c, :nt * 128])
        m["xsm"] = np.ascontiguousarray(x_sample[sl, 0, :])
        m["st_hg"] = np.ascontiguousarray(shg[:, sl].reshape(2, 96, 4096))
        m["st_gl"] = np.ascontiguousarray(sgl[:, sl].reshape(2, 96, 2048))
        m["st_re"] = np.ascontiguousarray(sre[:, sl].reshape(2, 16, 1024))
        m["st_im"] = np.ascontiguousarray(sim[:, sl].reshape(2, 16, 1024))
        in_maps.append(m)
    res = run_bass_kernel_spmd(nc, in_maps, core_ids=list(range(NCORES)))
    R = res.results
    for n in DBG_NAMES:
        DBG[n] = np.asarray(R[0]["dbg_" + n])
    y_prompt = np.stack([R[c]["yp"] for c in range(NCORES)], 0).reshape(8, nt * 128, 1024)
    y_sample = np.concatenate([R[c]["ysm"] for c in range(NCORES)], 0).reshape(128, 1, 1024)
    hgp = np.stack([R[c]["hgp"] for c in range(NCORES)], 1)
    glp = np.stack([R[c]["glp"] for c in range(NCORES)], 1)
    rep = np.stack([R[c]["rep"].reshape(2, 16, 64) for c in range(NCORES)], 1)
    imp = np.stack([R[c]["imp"].reshape(2, 16, 64) for c in range(NCORES)], 1)
    hgs = np.concatenate([R[c]["hgs"].reshape(2, 16, 6, 64, 64) for c in range(NCORES)], 1)
    gls = np.concatenate([R[c]["gls"].reshape(2, 16, 6, 32, 64) for c in range(NCORES)], 1)
    res_ = np.concatenate([R[c]["res_o"].reshape(2, 16, 16, 64) for c in range(NCORES)], 1)
    ims_ = np.concatenate([R[c]["ims_o"].reshape(2, 16, 16, 64) for c in range(NCORES)], 1)
    return tuple(np.ascontiguousarray(a, dtype=f32) for a in
                 (y_prompt, y_sample, hgp, glp, rep, imp, hgs, gls, res_, ims_))
```

```python
import math
import numpy as np
import ml_dtypes
from contextlib import ExitStack
import concourse.bass as bass
import concourse.mybir as mybir
from concourse.bass_utils import run_bass_kernel_spmd

F32 = mybir.dt.float32
BF16 = mybir.dt.bfloat16
I32 = mybir.dt.int32
AF = mybir.ActivationFunctionType
ALU = mybir.AluOpType
AX = mybir.AxisListType

NCORES = 8
D = 1024
INT = 3216
EPS = 1e-6
SAME_ENGINE_SYNC = True
DEBUG = False
NOSYNC_ENGS = ('pe',)
TW = [3, 1, 1, 2]
SCHED = True
SAMPLE_LAST = False
ONE_REC = True
LAST_FIRST = False
PRIO_ALPHA = 0.035
ENG_M4 = 'dve'
HILO = True
RMS1 = True
HN2 = True
FF1 = True
S5S_PRIV = True
SAMPLE_SPLIT = True
ENG_BD = 'dve'
ENG_S1 = 'pool'
ENG_S2 = 'pool'
ENG_S3 = 'pool'
ENG_KTS = 'dve'
ENG_XS = 'dve'
POW = False
CRITPRINT = False
TAGPRINT = None
TAGN = 400
ENG_KT = 'dve'
SWIN = 256
PE_GHZ = 2.4
FILL = True
FILL_NS = 130.0
FILL_N = 256
PE_MODE_PEN = 150.0
FILL_MAX = 48
VERBOSE = False
DBG_NAMES = []
DBG = {}

O_QA, O_FA, O_IA, O_ZA, O_QB, O_KB, O_VB, O_ZB, O_RB, O_UC, O_ZC = (
    0, 384, 768, 1152, 1536, 1728, 1920, 2304, 2688, 2704, 2960)

CF = {}
_off = 0
for _n, _w in [("identf", 128), ("ublk", 128), ("lblk", 128), ("ind", 2), ("jcol", 1), ("njcol", 1),
               ("maskB", 512), ("maskC", 512), ("selC", 128), ("dsel", 256)]:
    CF[_n] = (_off, _w)
    _off += _w
NCF = _off
CB = {"ident": (0, 128), "u128": (128, 128), "ones": (256, 128)}
NCB = 642


def make_consts():
    cf = np.zeros((128, NCF), np.float32)
    s = np.arange(128)
    same = (s[:, None] // 64) == (s[None, :] // 64)
    cf[:, CF["identf"][0]:CF["identf"][0] + 128] = np.eye(128)
    cf[:, CF["ublk"][0]:CF["ublk"][0] + 128] = ((s[:, None] <= s[None, :]) & same)
    cf[:, CF["lblk"][0]:CF["lblk"][0] + 128] = ((s[:, None] > s[None, :]) & same)
    cf[:, CF["ind"][0] + 0] = (s < 64)
    cf[:, CF["ind"][0] + 1] = (s >= 64)
    cf[:, CF["jcol"][0]] = s
    cf[:, CF["njcol"][0]] = -s
    mB = np.zeros((2, 64, 4, 8, 16), np.float32)
    for g2 in range(2):
        for qq in range(4):
            mB[g2, :, qq, 2 * qq + g2, :] = 1
    cf[:, CF["maskB"][0]:CF["maskB"][0] + 512] = mB.reshape(128, 512)
    mC = np.zeros((8, 16, 4, 2, 64), np.float32)
    for g2 in range(2):
        for qq in range(4):
            mC[2 * qq + g2, :, qq, g2, :] = 1
    cf[:, CF["maskC"][0]:CF["maskC"][0] + 512] = mC.reshape(128, 512)
    sC = np.zeros((8, 16, 4, 2, 16), np.float32)
    for g2 in range(2):
        for qq in range(4):
            for c in range(16):
                sC[2 * qq + g2, c, qq, g2, c] = 1
    cf[:, CF["selC"][0]:CF["selC"][0] + 128] = sC.reshape(128, 128)
    dS = np.zeros((128, 16, 16), np.float32)
    for b in range(16):
        dS[:, b, b] = 1
    cf[:, CF["dsel"][0]:CF["dsel"][0] + 256] = dS.reshape(128, 256)
    cb = np.zeros((128, NCB), np.float32)
    cb[:, 0:128] = np.eye(128)
    cb[:, 128:256] = (s[:, None] <= s[None, :])
    cb[:, 256:384] = 1.0
    cb[:, 384:512] = cf[:, CF["ublk"][0]:CF["ublk"][0] + 128]
    cb[:, 512:640] = cf[:, CF["lblk"][0]:CF["lblk"][0] + 128]
    cb[:, 640:642] = cf[:, CF["ind"][0]:CF["ind"][0] + 2]
    return cf, cb.astype(ml_dtypes.bfloat16)


class Buf:
    def __init__(self, t, name):
        self.t = t
        self.name = name
        self.lw = None
        self.rd = []

    def __getitem__(self, idx):
        return V(self.t[idx], self)


class V:
    def __init__(self, ap, buf):
        self.ap = ap
        self.buf = buf

    def re(self, pat, **kw):
        return V(self.ap.rearrange(pat, **kw), self.buf)

    def bl(self, n):
        sh = list(self.ap.shape)
        return V(self.ap.unsqueeze(len(sh)).broadcast_to(sh + [n]), self.buf)

    def bm(self, n):
        sh = list(self.ap.shape)
        return V(self.ap.unsqueeze(1).broadcast_to([sh[0], n] + sh[1:]), self.buf)

    def __getitem__(self, idx):
        return V(self.ap[idx], self.buf)


class Eng:
    def __init__(self, name, h, sem):
        self.name = name
        self.h = h
        self.sem = sem
        self.count = 0
        self.known = {}


class Ctx:
    def __init__(self, nc, es):
        self.nc = nc
        self.es = es
        self.engs = {}
        for name, h in [("pe", nc.tensor), ("act", nc.scalar), ("dve", nc.vector), ("pool", nc.gpsimd)]:
            self.engs[name] = Eng(name, h, es.enter_context(nc.semaphore("s_" + name)))
        self.dq = {}
        for name, h, n in [("sync", nc.sync, 12), ("gq", nc.gpsimd, 8)]:
            sems = [es.enter_context(nc.semaphore("d_%s%d" % (name, i))) for i in range(n)]
            self.dq[name] = dict(h=h, sems=sems, cnt=[0] * n, i=0, known={})
        self.engs["sync"] = Eng("sync", nc.sync, None)
        self.dq["sync"]["eng"] = self.engs["sync"]
        self.dq["gq"]["eng"] = self.engs["pool"]
        self.nbuf = 0
        self.bg = None
        self.recording = None
        self.filler = None
        self.nfill = 0
        self.tag = ""
        self.in_bg = False
        self.nops = 0

    def sb(self, name, shape, dt):
        t = self.es.enter_context(self.nc.sbuf_tensor(name, shape, dt))
        return Buf(t, name)

    def ps(self, name, shape, dt):
        t = self.es.enter_context(self.nc.psum_tensor(name, shape, dt))
        return Buf(t, name)

    def dram(self, name, shape, dt, kind):
        t = self.nc.dram_tensor(name, shape, dt, kind=kind)
        return Buf(t.ap(), name)

    def _waits(self, eng, reads, writes):
        deps = []
        for v in reads:
            b = v.buf
            if b.lw is not None:
                deps.append(b.lw)
        for v in writes:
            b = v.buf
            if b.lw is not None:
                deps.append(b.lw)
            deps.extend(b.rd)
        need = {}
        for (sem, val, owner) in deps:
            if owner is eng and (not SAME_ENGINE_SYNC or eng.name in NOSYNC_ENGS):
                continue
            k = id(sem)
            if eng.known.get(k, 0) >= val:
                continue
            if k not in need or need[k][1] < val:
                need[k] = (sem, val)
        for k, (sem, val) in need.items():
            eng.h.wait_ge(sem, val)
            eng.known[k] = val

    def op(self, ename, fn, reads, writes, cost=None, func=None):
        if self.recording is not None:
            if cost is None:
                n = 1
                for d in list(writes[0].ap.shape)[1:]:
                    n *= int(d)
                cost = (260 + 3.3 * n) if ename == "pool" else (75 + 1.25 * n)
            self.recording.append(dict(kind="op", eng=ename, fn=fn, reads=list(reads), writes=list(writes),
                                       cost=cost, func=func, tag=self.tag))
            return
        if self.bg is not None and not self.in_bg:
            self.nops += 1
            if self.nops % 3 == 0:
                self.in_bg = True
                next(self.bg, None)
                self.in_bg = False
        eng = self.engs[ename]
        self._waits(eng, reads, writes)
        ins = fn()
        eng.count += 1
        ins.then_inc(eng.sem, 1)
        ev = (eng.sem, eng.count, eng)
        for v in writes:
            v.buf.lw = ev
            v.buf.rd = []
        for v in reads:
            if v.buf.lw is not ev:
                v.buf.rd.append(ev)

    def dma(self, qname, out, in_, extra_reads=(), **kw):
        if self.recording is not None:
            n = 1
            for d in list(out.ap.shape):
                n *= int(d)
            self.recording.append(dict(kind="dma", eng=("sync" if qname == "sync" else "pool"), q=qname, out=out, in_=in_,
                                       kw=kw, reads=[in_] + list(extra_reads), writes=[out], cost=2500 + n * 4 / 100.0,
                                       func=None, tag=self.tag, extra=list(extra_reads)))
            return
        q = self.dq[qname]
        eng = q["eng"]
        slot = q["i"] % len(q["sems"])
        q["i"] += 1
        sem = q["sems"][slot]
        if q["cnt"][slot] > 0:
            k = id(sem)
            val = 16 * q["cnt"][slot]
            if eng.known.get(k, 0) < val:
                eng.h.wait_ge(sem, val)
                eng.known[k] = val
        self._waits(eng, [in_] + list(extra_reads), [out])
        q["h"].dma_start(out=out.ap, in_=in_.ap, **kw).then_inc(sem, 16)
        q["cnt"][slot] += 1
        ev = (sem, 16 * q["cnt"][slot], None)
        out.buf.lw = ev
        out.buf.rd = []
        in_.buf.rd.append(ev)
        for v in extra_reads:
            v.buf.rd.append(ev)

    def schedule_and_emit(self, window=96):
        ops = self.recording
        self.recording = None
        n = len(ops)
        preds = [set() for _ in range(n)]
        lw = {}
        rd = {}
        for i, o in enumerate(ops):
            for v in o["reads"]:
                b = id(v.buf)
                if b in lw:
                    preds[i].add(lw[b])
            for v in o["writes"]:
                b = id(v.buf)
                if b in lw:
                    preds[i].add(lw[b])
                for r in rd.get(b, ()):
                    preds[i].add(r)
            for v in o["writes"]:
                b = id(v.buf)
                lw[b] = i
                rd[b] = []
            for v in o["reads"]:
                b = id(v.buf)
                if lw.get(b) != i:
                    rd.setdefault(b, []).append(i)
            preds[i].discard(i)
        succs = [[] for _ in range(n)]
        indeg = [0] * n
        for i in range(n):
            indeg[i] = len(preds[i])
            for p in preds[i]:
                succs[p].append(i)
        tail = [0.0] * n
        for i in range(n - 1, -1, -1):
            m = 0.0
            for sx in succs[i]:
                if tail[sx] > m:
                    m = tail[sx]
            tail[i] = m + ops[i]["cost"] + (0.0 if ops[i]["eng"] == "pe" else 150.0)
        import heapq
        ready = [i for i in range(n) if indeg[i] == 0]
        heapq.heapify(ready)
        eng_free = {}
        fin = [0.0] * n
        act_set = [None]
        pe_mode = [None]
        order = []

        def tset(o):
            f = o["func"]
            if f is None or o["eng"] != "act" or isinstance(f, tuple):
                return None
            if f == AF.Ln:
                return "B"
            if f == AF.Tanh:
                return "A"
            if f == AF.Sin:
                return "C"
            return None
        while ready:
            cand = heapq.nsmallest(window, ready)
            best = None
            for i in cand:
                o = ops[i]
                e = o["eng"]
                st = eng_free.get(e, 0.0)
                for p in preds[i]:
                    if ops[p]["eng"] == e and ops[p]["kind"] == "op":
                        lat = 0.0 if e == "pe" else 100.0
                    else:
                        lat = 200.0
                    t = fin[p] + lat
                    if t > st:
                        st = t
                ts_ = tset(o)
                pen = 1300.0 if (ts_ is not None and act_set[0] is not None and ts_ != act_set[0]) else 0.0
                if e == "pe" and isinstance(o["func"], tuple) and pe_mode[0] is not None and o["func"] != pe_mode[0]:
                    pen = PE_MODE_PEN
                key = (st + pen - PRIO_ALPHA * tail[i], i)
                if best is None or key < best[0]:
                    best = (key, i, st, pen)
            _, i, st, pen = best
            ready.remove(i)
            heapq.heapify(ready)
            o = ops[i]
            e = o["eng"]
            if e == "pe" and self.filler is not None:
                gap = st - eng_free.get("pe", 0.0)
                nf = int(gap // FILL_NS)
                if nf >= 2:
                    order.append(("fill", min(nf, FILL_MAX)))
            if o["kind"] == "dma":
                eng_free[e] = st + 60.0
                fin[i] = st + o["cost"]
            else:
                fin[i] = st + pen + o["cost"]
                eng_free[e] = fin[i]
                ts_ = tset(o)
                if ts_ is not None:
                    act_set[0] = ts_
                if e == "pe" and isinstance(o["func"], tuple):
                    pe_mode[0] = o["func"]
            order.append(i)
            for sidx in succs[i]:
                indeg[sidx] -= 1
                if indeg[sidx] == 0:
                    heapq.heappush(ready, sidx)
        self.sim_time = max(fin) if n else 0.0
        self.crit = max(tail) if n else 0.0
        if TAGPRINT:
            rows = []
            for j, o in enumerate(ops):
                if o["tag"] == TAGPRINT:
                    rows.append((fin[j] - o["cost"], o["eng"], o["cost"], o["writes"][0].buf.name))
            rows.sort()
            for r in rows[:TAGN]:
                print("   %9.1f %5s %7.0f %s" % (r[0] / 1e3, r[1], r[2], r[3]))
        if CRITPRINT:
            i = max(range(n), key=lambda j: tail[j])
            path = []
            while True:
                path.append(i)
                if not succs[i]:
                    break
                i = max(succs[i], key=lambda j: tail[j])
            import collections
            cnt = collections.Counter()
            for j in path:
                o = ops[j]
                cnt[(o["eng"], o["writes"][0].buf.name)] += o["cost"] + 150
            print("critical path ops", len(path))
            for k, v in cnt.most_common(40):
                print("   ", k, round(v / 1e3, 1))
            mid = len(path) // 2
            for j in path[mid:mid + 70]:
                o = ops[j]
                print("      ", o["tag"], o["eng"], round(o["cost"]), o["writes"][0].buf.name, [r.buf.name for r in o["reads"]])
        tags = {}
        for i, o in enumerate(ops):
            t = o["tag"]
            st_ = fin[i] - o["cost"]
            if t not in tags:
                tags[t] = [st_, fin[i], 0]
            tags[t][0] = min(tags[t][0], st_)
            tags[t][1] = max(tags[t][1], fin[i])
            tags[t][2] += 1
        self.sim_tags = {k: (round(v[0] / 1e3, 1), round(v[1] / 1e3, 1), v[2]) for k, v in tags.items()}
        busy = {}
        for o in ops:
            if o["kind"] == "op":
                busy[o["eng"]] = busy.get(o["eng"], 0.0) + o["cost"]
        self.sim_busy = {k: round(v / 1e3, 1) for k, v in busy.items()}
        for i in order:
            if isinstance(i, tuple):
                for k in range(i[1]):
                    self.filler(k)
                self.nfill += i[1]
                continue
            o = ops[i]
            if o["kind"] == "dma":
                self.dma(o["q"], o["out"], o["in_"], extra_reads=o.get("extra", ()), **o["kw"])
            else:
                self.op(o["eng"], o["fn"], o["reads"], o["writes"])

    def finish(self):
        for q in self.dq.values():
            for sem, c in zip(q["sems"], q["cnt"]):
                if c > 0:
                    q["eng"].h.wait_ge(sem, 16 * c)
        s = self.engs["sync"]
        for n in ("pe", "act", "dve", "pool"):
            e = self.engs[n]
            if e.count > 0:
                s.h.wait_ge(e.sem, e.count)

    @staticmethod
    def _n(v):
        n = 1
        for d in list(v.ap.shape)[1:]:
            n *= int(d)
        return n

    def mm(self, out, lhsT, rhs, start=True, stop=True):
        c = 20 + max(64, self._n(out)) / PE_GHZ
        if lhsT.ap.dtype == F32:
            c *= 4.5
        def _r(x):
            return 32 if x <= 32 else (64 if x <= 64 else 128)
        mode = ("pe", _r(int(lhsT.ap.shape[0])), _r(self._n(lhsT)))
        self.op("pe", lambda: self.nc.tensor.matmul(out.ap, lhsT=lhsT.ap, rhs=rhs.ap, start=start, stop=stop),
                [lhsT, rhs], [out], cost=c, func=mode)

    def tr(self, out, in_, ident):
        def _r(x):
            return 32 if x <= 32 else (64 if x <= 64 else 128)
        mode = ("pe", _r(int(in_.ap.shape[0])), _r(self._n(in_)))
        self.op("pe", lambda: self.nc.tensor.transpose(out.ap, in_.ap, ident.ap), [in_, ident], [out], cost=110.0,
                func=mode)

    def act(self, out, in_, func, scale=1.0, bias=None, accum=None):
        rd = [in_]
        kw = {}
        if isinstance(scale, V):
            rd.append(scale)
            kw["scale"] = scale.ap
        else:
            kw["scale"] = float(scale)
        if bias is not None:
            if isinstance(bias, V):
                rd.append(bias)
                kw["bias"] = bias.ap
            else:
                kw["bias"] = float(bias)
        wr = [out]
        if accum is not None:
            kw["accum_out"] = accum.ap
            wr.append(accum)
        self.op("act", lambda: self.nc.scalar.activation(out=out.ap, in_=in_.ap, func=func, **kw), rd, wr,
                cost=250 + 1.0 * self._n(out), func=func)

    def _e(self, ename):
        return {"dve": self.nc.vector, "pool": self.nc.gpsimd}[ename]

    def tt(self, ename, out, a, b, op):
        cost = None
        if ename == "dve" and out.ap.dtype == BF16 and a.ap.dtype == BF16 and b.ap.dtype == BF16:
            cost = 75 + 0.65 * self._n(out)
        self.op(ename, lambda: self._e(ename).tensor_tensor(out=out.ap, in0=a.ap, in1=b.ap, op=op), [a, b], [out],
                cost=cost)

    def ts(self, ename, out, a, s1, op0, s2=None, op1=None):
        rd = [a]
        k1 = s1.ap if isinstance(s1, V) else float(s1)
        if isinstance(s1, V):
            rd.append(s1)
        kw = {}
        if op1 is not None:
            kw["op1"] = op1
            k2 = s2.ap if isinstance(s2, V) else float(s2)
            if isinstance(s2, V):
                rd.append(s2)
        else:
            k2 = None
        self.op(ename, lambda: self._e(ename).tensor_scalar(out=out.ap, in0=a.ap, scalar1=k1, scalar2=k2, op0=op0, **kw),
                rd, [out])

    def stt(self, out, a, s, b, op0, op1):
        rd = [a, b]
        ks = s.ap if isinstance(s, V) else float(s)
        if isinstance(s, V):
            rd.append(s)
        self.op("dve", lambda: self.nc.vector.scalar_tensor_tensor(out=out.ap, in0=a.ap, scalar=ks, in1=b.ap,
                                                                    op0=op0, op1=op1), rd, [out])

    def cp(self, ename, out, in_):
        if ename == "act":
            self.act(out, in_, AF.Copy)
        else:
            self.op(ename, lambda: self._e(ename).tensor_copy(out=out.ap, in_=in_.ap), [in_], [out])

    def red(self, out, in_, op=ALU.add):
        self.op("dve", lambda: self.nc.vector.tensor_reduce(out=out.ap, in_=in_.ap, axis=AX.X, op=op), [in_], [out])

    def memset(self, ename, out, val):
        self.op(ename, lambda: self._e(ename).memset(out.ap, float(val)), [], [out])


LN_HALF = math.log(0.5)
LN_QS = math.log(32 ** -0.5)
GC = math.sqrt(2.0 / math.pi)
TWO_PI = 2.0 * math.pi


def build(nt=16, nl=2):
    nc = bass.Bass("TRN2", target_bir_lowering=False)
    es = ExitStack()
    with es:
        cx = Ctx(nc, es)
        NTOK = nt * 128

        def din(name, shape, dt=F32):
            return cx.dram(name, shape, dt, "ExternalInput")

        def dout(name, shape):
            return cx.dram(name, shape, F32, "ExternalOutput")

        xp = din("xp", [NTOK, D])
        xsm = din("xsm", [16, D])
        st_hg = din("st_hg", [2, 96, 4096])
        st_gl = din("st_gl", [2, 96, 2048])
        st_re = din("st_re", [2, 16, 1024])
        st_im = din("st_im", [2, 16, 1024])
        norm_gT = din("norm_gT", [128, 16])
        w_in = din("w_in", [2, D, INT])
        lbl = din("lbl", [2, 384])
        hg_g = din("hg_g", [2, 384])
        w2 = din("w2", [2, 16, 192])
        b2 = din("b2", [2, 192])
        gla_g = din("gla_g", [2, 384])
        A_re = din("A_re", [2, 1024])
        A_im = din("A_im", [2, 1024])
        B_re = din("B_re", [2, 16, 64, 16])
        B_im = din("B_im", [2, 16, 64, 16])
        C_re = din("C_re", [2, 16, 16, 64])
        C_im = din("C_im", [2, 16, 16, 64])
        Dv = din("Dv", [2, 256])
        ldt = din("ldt", [2, 16])
        wglu = din("wglu", [2, 256, 256])
        bglu = din("bglu", [2, 256])
        w_out = din("w_out", [2, D, D])
        gfin = din("gfin", [D])
        cf_d = din("cf", [128, NCF])
        cb_d = din("cb", [128, NCB], BF16)

        yp = dout("yp", [NTOK, D])
        ysm = dout("ysm", [16, D])
        hgp = dout("hgp", [2, 6, 64, 64])
        glp = dout("glp", [2, 6, 32, 64])
        rep = dout("rep", [2, 1024])
        imp = dout("imp", [2, 1024])
        hgs = dout("hgs", [2, 96, 4096])
        gls = dout("gls", [2, 96, 2048])
        res_o = dout("res_o", [2, 16, 1024])
        ims_o = dout("ims_o", [2, 16, 1024])

        y0 = cx.dram("y0", [NTOK, D], F32, "Internal")
        scr_s = [[cx.dram("scr_s%d_%d" % (mi, j), [16, 384 if (mi == 0 or j == 3) else 192], F32, "Internal")
                  for j in range(4)] for mi in range(2)]
        scr_o = cx.dram("scr_o", [2, 96, 64], F32, "Internal")

        sb, ps = cx.sb, cx.ps

        dbgbuf = [None]

        def dbg(name, v):
            if not DEBUG or name in DBG_NAMES:
                return
            DBG_NAMES.append(name)
            sh = list(v.ap.shape)
            n = int(np.prod(sh[1:]))
            P_ = sh[0]
            if dbgbuf[0] is None:
                dbgbuf[0] = sb("dbgbuf", [128, 1024], F32)
            d = cx.dram("dbg_" + name, [P_, n], F32, "ExternalOutput")
            dst = dbgbuf[0][0:P_, 0:n]
            if len(sh) == 3:
                dst = dst.re("p (a b) -> p a b", a=sh[1])
            elif len(sh) == 4:
                dst = dst.re("p (a b c) -> p a b c", a=sh[1], b=sh[2])
            cx.cp("dve", dst, v)
            cx.dma("gq", V(d.t, d), dbgbuf[0][0:P_, 0:n])

        cf = sb("cf_sb", [128, 388], F32)
        cb = sb("cb_sb", [128, NCB], BF16)
        cx.dma("sync", cf[:, :], cf_d[:, 0:388])
        cx.dma("sync", cb[:, :], cb_d[:, :])

        def cfv(name, rows=slice(0, 128)):
            o, w = CF[name]
            return cf[rows, o:o + w]

        ident = cb[:, 0:128]
        u128 = cb[:, 128:256]
        ones_row = cb[0:1, 256:384]
        ublk_b = cb[:, 384:512]
        lblk_b = cb[:, 512:640]
        ind_b = cb[:, 640:642]
        identf = cfv("identf")
        ublk = cfv("ublk")
        lblk = cfv("lblk")
        ind = cfv("ind")

        win = sb("win", [128, 8, INT], BF16)
        wout = sb("wout", [128, 8, D], BF16)
        wink = [Buf(win.t[:, kc, :], "win%d" % kc) for kc in range(8)]
        woutk = [Buf(wout.t[:, kc, :], "wout%d" % kc) for kc in range(8)]
        SW = 804
        gT = sb("gT", [128, 16], F32)
        cx.dma("sync", gT[:, :], norm_gT[:, :])
        Ere = sb("Ere", [128, 1024], F32)
        Eim = sb("Eim", [128, 1024], F32)
        Fre = sb("Fre", [128, 8, 128], BF16)
        Fim = sb("Fim", [128, 8, 128], BF16)
        f127 = sb("f127", [128, 2, 8], F32)
        a1t = sb("a1t", [128, 2, 8], F32)
        A128 = sb("A128", [128, 2, 8], F32)
        coT = sb("coT", [128, 2, 8], F32)
        BD = sb("BD", [128, 2, 2, 512], BF16)
        Cblk = sb("Cblk", [128, 2, 8, 32], BF16)
        Ddiag = sb("Ddiag", [128, 256], BF16)
        wgl = sb("wgl", [128, 2, 256], BF16)
        bgl = sb("bgl", [1, 256], BF16)
        w2b = sb("w2b", [16, 192], BF16)
        b2b = sb("b2b", [1, 192], BF16)
        c0r = sb("c0r", [128, 384], F32)
        c1r = sb("c1r", [128, 384], F32)
        gAr = sb("gAr", [128, 384], BF16)
        gBr = sb("gBr", [128, 384], BF16)
        gfr = sb("gfr", [128, D], BF16)

        class NS:
            pass

        def f32(name, shape):
            return sb(name, shape, F32)

        def b16(name, shape):
            return sb(name, shape, BF16)

        xin = [f32("xin0", [128, D]), f32("xin1", [128, D]), f32("xin2", [128, D])]
        cx.dma("sync", xin[2][:, :], V(gfin.t.partition_broadcast(128), gfin))
        cx.cp("dve", gfr[:, :], xin[2][:, :])
        stage = xin
        nhalf = f32("nhalf", [128, 8])
        cx.memset("dve", nhalf[:, :], -0.5)
        xsr = f32("xsr", [16, D])
        xsb = b16("xsb", [128, D])
        hT = b16("hT", [128, 8, 128])
        THA = f32("THA", [128, 512])
        smallA = f32("smallA", [128, 8])
        smallO = f32("smallO", [128, 8])
        rbT = b16("rbT", [16, 128])
        mixb = [b16("mix0", [128, D]), b16("mix1", [128, D])]
        mixT = b16("mixT", [128, 8, 128])

        def mkset(i):
            a = NS()
            for n in ("Q2", "FF", "LOGF", "GA", "QKB", "GB"):
                setattr(a, n, f32("%s%d" % (n, i), [128, 384]))
            a.GZC = f32("GZC%d" % i, [128, 256])
            a.LG = f32("LG%d" % i, [128, 192])
            a.va = b16("va%d" % i, [128, 384])
            a.vb = b16("vb%d" % i, [128, 384])
            a.uT = b16("uT%d" % i, [128, 256])
            return a
        sets = [mkset(0), mkset(1)]

        def mkrec(nm, K):
            b = NS()
            HK = 6 * K
            b.E1, b.E2, b.E3 = (f32("%sE%d" % (nm, j), [128, HK]) for j in range(3))
            b.OSB = f32(nm + "OSB", [128, 384])
            b.TH = f32(nm + "TH", [128, 384])
            b.qt, b.kt, b.kh = (b16("%s%s" % (nm, j), [128, HK]) for j in ("qt", "kt", "kh"))
            b.qTs = b16(nm + "qTs", [K, 768])
            b.kTs = b16(nm + "kTs", [K, 768])
            b.attS = b16(nm + "attS", [128, 768])
            b.small = f32(nm + "small", [128, 32])
            b.Dd = f32(nm + "Dd", [K, 12])
            b.S = f32(nm + "S", [K, 384])
            b.Sbf = [b16(nm + "Sbf%d" % j, [K, 384]) for j in range(2)]
            b.p = [ps(nm + "p%d" % j, [128, 512], F32) for j in range(2)]
            return b
        HB = mkrec("H", 64)
        GBs = mkrec("G", 32)
        fill_bank = None
        if FILL:
            fill_bank = GBs.p[1]
            GBs.p = [GBs.p[0], GBs.p[0]]
        SB = NS()
        SB.t = [f32("St%d" % j, [128, 512]) for j in range(4)]
        SB.TH = f32("STH", [128, 256])
        SB.ysb = SB.TH
        SB.wre = b16("Swre", [128, 512])
        SB.wim = b16("Swim", [128, 512])
        SB.zcre, SB.zcim = (b16("S" + n, [128, 4, 128]) for n in ("zcre", "zcim"))
        SB.xre, SB.nxim = (f32("S" + n, [128, 4, 128]) for n in ("xre", "nxim"))
        SB.tb = [b16("Stb%d" % j, [128, 512]) for j in range(4)]
        SB.z127 = f32("Sz127", [128, 2, 4])
        SB.xl = f32("Sxl", [128, 2, 4])
        SB.xreb = b16("Sxreb", [128, 4, 128])
        SB.nximb = b16("Snximb", [128, 4, 128])
        SB.ycb = b16("Sycb", [128, 256])
        SB.ycT = b16("SycT", [128, 256])
        SB.cre = f32("Scre", [128, 8])
        SB.cim = f32("Scim", [128, 8])
        SB.small = f32("Ssmall", [128, 8])
        SB.p = [ps("Sp%d" % j, [128, 512], F32) for j in range(2)]
        pA = [ps("pA%d" % j, [128, 512], F32) for j in range(2)]

        def bfv(bank):
            return V(bank.t[:, :].bitcast(BF16), bank)

        def run(g):
            for _ in g:
                pass

        PW = [V(xin[0].t[:, 0:512], xin[0]), V(xin[0].t[:, 512:1024], xin[0]),
              V(xin[1].t[:, 0:512], xin[1]), V(xin[1].t[:, 512:1024], xin[1]),
              SB.t[0][:, :], SB.t[1][:, :], SB.t[2][:, :], SB.t[3][:, :],
              V(xin[2].t[:, 0:512], xin[2]), V(xin[2].t[:, 512:1024], xin[2]),
              SB.xre[:, :, :].re("p a b -> p (a b)"), SB.nxim[:, :, :].re("p a b -> p (a b)")]
        tmpi = V(SB.xreb.t[:, :, :].rearrange("p a b -> p (a b)").bitcast(I32)[:, 0:256], SB.xreb)

        def fview(buf):
            t = buf.t
            ap = t[:, :] if len(t.shape) == 2 else t[:, :, :].rearrange("p a b -> p (a b)")
            return V(ap.bitcast(F32), buf)
        stages = [fview(x) for x in (mixb[0], mixb[1], xsb, hT, mixT)]
        SW2 = 402

        def load_weights(l):
            for kc in range(8):
                cx.dma("gq", wink[kc][:, :], w_in[l, kc * 128:(kc + 1) * 128, :],
                       extra_reads=[b_[0:1, 0:1] for b_ in prep_state.get("bufs", [])] if kc == 0 else ())
                cx.act(wink[kc][:, :], wink[kc][:, :], AF.Copy, scale=gT[:, l * 8 + kc:l * 8 + kc + 1])
                yield
            for kc in range(8):
                cx.dma("gq", woutk[kc][:, :], w_out[l, kc * 128:(kc + 1) * 128, :])
                yield

        prep_state = {}

        def prep_layer(l):
            pm = SB.p[0]
            s0, s1 = sets
            L_lb0, L_lb1, L_hg, L_gla, L_w2, L_b2, L_bglu = s0.Q2, s0.FF, s0.LOGF, s0.GA, s0.QKB, s0.GB, s0.GZC
            L_wg = [s1.Q2, s1.FF]
            L_Dv, L_sC = s1.LOGF, s1.GA
            L_B = [s1.QKB, s1.GB]
            L_C = [HB.E1, HB.E2]
            mB, mC = fview(mixb[0]), fview(mixb[1])
            tlb, pad = HB.OSB, HB.TH
            cx.dma("sync", L_lb0[:, 0:384], V(lbl.t[0].partition_broadcast(128), lbl))
            cx.dma("sync", L_lb1[:, 0:384], V(lbl.t[1].partition_broadcast(128), lbl))
            cx.dma("sync", L_hg[:, 0:384], V(hg_g.t[l].partition_broadcast(128), hg_g))
            cx.dma("sync", L_gla[:, 0:384], V(gla_g.t[l].partition_broadcast(128), gla_g))
            cx.dma("sync", L_w2[0:16, 0:192], w2[l, :, :])
            cx.dma("sync", L_b2[0:1, 0:192], b2[l:l + 1, :])
            cx.dma("sync", L_bglu[0:1, 0:256], bglu[l:l + 1, :])
            for kc in range(2):
                cx.dma("sync", L_wg[kc][:, 0:256], wglu[l, kc * 128:(kc + 1) * 128, :])
            cx.dma("sync", L_Dv[:, 0:256], V(Dv.t[l].partition_broadcast(128), Dv))
            cx.dma("sync", mB[:, 0:512], cf_d[:, CF["maskB"][0]:CF["maskB"][0] + 512])
            cx.dma("sync", mC[:, 0:512], cf_d[:, CF["maskC"][0]:CF["maskC"][0] + 512])
            cx.dma("sync", L_sC[:, 0:128], cf_d[:, CF["selC"][0]:CF["selC"][0] + 128])
            for ri, Bsrc in enumerate((B_re, B_im)):
                cx.dma("sync", L_B[ri][:, 0:128].re("p (q c) -> p q c", q=8),
                       V(Bsrc.t[l].rearrange("(q g2) p c -> (g2 p) q c", g2=2), Bsrc))
            for ri, Csrc in enumerate((C_re, C_im)):
                cx.dma("sync", L_C[ri][:, 0:128].re("p (k e) -> p k e", k=2),
                       V(Csrc.t[l].rearrange("(k g) c e -> (g c) k e", k=2), Csrc))
            gtl = gen_table_loads(l)
            prep_state["bufs"] = gtl + [L_lb0, L_lb1, L_hg, L_gla, L_w2, L_b2, L_bglu, L_wg[0], L_wg[1], L_Dv, L_sC,
                                  L_B[0], L_B[1], L_C[0], L_C[1], mixb[0], mixb[1]]
            yield
            if l == 0:
                cx.memset("dve", c0r[:, :], 0.5)
                cx.memset("dve", c1r[:, :], 0.5)
            else:
                cx.tt("dve", tlb[:, 0:384], L_lb1[:, 0:384], L_lb0[:, 0:384], ALU.subtract)
                cx.act(tlb[:, 0:384], tlb[:, 0:384], AF.Tanh, scale=0.5)
                cx.ts("dve", c0r[:, :], tlb[:, 0:384], 0.25, ALU.mult, 0.75, ALU.add)
                cx.ts("dve", c1r[:, :], tlb[:, 0:384], -0.25, ALU.mult, 0.25, ALU.add)
            cx.ts("dve", gAr[:, :], L_hg[:, 0:384], 0.5, ALU.mult)
            cx.ts("dve", gBr[:, :], L_gla[:, 0:384], 0.5, ALU.mult)
            cx.cp("dve", w2b[:, :], L_w2[0:16, 0:192])
            cx.cp("dve", b2b[:, :], L_b2[0:1, 0:192])
            cx.cp("dve", bgl[:, :], L_bglu[0:1, 0:256])
            for kc in range(2):
                cx.cp("dve", wgl[:, kc, :], L_wg[kc][:, 0:256])
            for kc in range(2):
                cx.tt("dve", Ddiag[:, kc * 128:(kc + 1) * 128], identf, L_Dv[:, kc * 128:(kc + 1) * 128], ALU.mult)
            selb = SB.wim
            cx.cp("dve", selb[:, 0:128], L_sC[:, 0:128])
            padb = SB.wre
            for ri in range(2):
                Bn = L_B[ri]
                for q in range(8):
                    kc, qq = q // 4, q % 4
                    cx.tt(ENG_BD, pad[:, 0:128].re("p (g c) -> p g c", g=8),
                          mB[:, qq * 128:(qq + 1) * 128].re("p (g c) -> p g c", g=8),
                          Bn[:, q * 16:(q + 1) * 16].bm(8), ALU.mult)
                    cx.cp(ENG_BD, padb[:, 0:128], pad[:, 0:128])
                    cx.mm(pm[:, 0:128], padb[:, 0:128], ident)
                    cx.cp("act", BD[:, kc, ri, qq * 128:(qq + 1) * 128], pm[:, 0:128])
            for ri in range(2):
                Cn = L_C[ri]
                for q in range(8):
                    kc, qq = q // 4, q % 4
                    cx.tt(ENG_BD, pad[:, 0:128].re("p (g e) -> p g e", g=2),
                          mC[:, qq * 128:(qq + 1) * 128].re("p (g e) -> p g e", g=2),
                          Cn[:, kc * 64:(kc + 1) * 64].bm(2), ALU.mult)
                    cx.cp(ENG_BD, padb[:, 0:128], pad[:, 0:128])
                    cx.mm(pm[:, 0:32], padb[:, 0:128], selb[:, qq * 32:(qq + 1) * 32])
                    cx.act(Cblk[:, ri, q, :], pm[:, 0:32], AF.Copy, scale=(1.0 if ri == 0 else -1.0))
            gen_tables(l)

        def reduce_angle(ang, tmpf, n):
            cx.ts("dve", tmpf[:, 0:n], ang[:, 0:n], 1.0 / TWO_PI, ALU.mult)
            for h2 in range(n // 256):
                sl = slice(h2 * 256, (h2 + 1) * 256)
                cx.cp("dve", tmpi, tmpf[:, sl])
                cx.cp("dve", tmpf[:, sl], tmpi)
            cx.stt(ang[:, 0:n], tmpf[:, 0:n], -TWO_PI, ang[:, 0:n], ALU.mult, ALU.add)
            cx.ts("dve", ang[:, 0:n], ang[:, 0:n], math.pi, ALU.min, -math.pi, ALU.max)

        def sincos(ang, sn, cs, tmpf):
            cx.act(sn[:, :], ang[:, :], AF.Sin)
            cx.act(tmpf[:, :], ang[:, :], AF.Abs)
            cx.act(cs[:, :], tmpf[:, :], AF.Sin, scale=-1.0, bias=math.pi / 2)

        fm = f32("fmtmp", [128, 12, 8])
        tmpi8 = sb("tmpi8", [128, 8], I32)

        def gen_table_loads(l):
            lr, li, dtb = PW[0], PW[1], PW[2]
            cx.dma("sync", fm[:, 0, :], V(A_re.t[l].rearrange("(q r) -> r q", q=8), A_re), allow_slow_non_contiguous=True)
            cx.dma("sync", fm[:, 1, :], V(A_im.t[l].rearrange("(q r) -> r q", q=8), A_im), allow_slow_non_contiguous=True)
            for g2 in range(2):
                cx.dma("sync", fm[g2 * 64:(g2 + 1) * 64, 2, :],
                       V(ldt.t[l].rearrange("(q g) -> g q", g=2)[g2].partition_broadcast(64), ldt),
                       allow_slow_non_contiguous=True)
            cx.dma("sync", HB.small[:, 0:16], V(ldt.t[l].partition_broadcast(128), ldt))
            for hf in range(2):
                sl = slice(hf * 512, (hf + 1) * 512)
                lr_, li_ = (lr, li) if hf == 0 else (dtb, THA[:, :])
                cx.dma("sync", lr_[:, :], V(A_re.t[l, sl].partition_broadcast(128), A_re))
                cx.dma("sync", li_[:, :], V(A_im.t[l, sl].partition_broadcast(128), A_im))
            return [fm, HB.small, xin[0], xin[1], THA]

        def gen_tables(l):
            lr, li, dtb, lrdt, lidt, ang, tf, mg, cs, sn, dg, onesf = PW
            pm = SB.p[0]
            small = HB.small
            jcol = cfv("jcol")
            njcol = cfv("njcol")
            lrT, liT, dtT, lrdtT, angT, snT, csT, mgT, tA, tB, rec = [fm[:, k, :] for k in range(11)]
            cx.act(dtT, dtT, AF.Exp)
            cx.tt("dve", lrdtT, lrT, dtT, ALU.mult)
            cx.tt("dve", angT, liT, dtT, ALU.mult)
            cx.ts("dve", tA, angT, 1.0 / TWO_PI, ALU.mult)
            cx.cp("dve", tmpi8[:, :], tA)
            cx.cp("dve", tA, tmpi8[:, :])
            cx.stt(angT, tA, -TWO_PI, angT, ALU.mult, ALU.add)
            cx.ts("dve", angT, angT, math.pi, ALU.min, -math.pi, ALU.max)
            cx.act(snT, angT, AF.Sin)
            cx.act(tA, angT, AF.Abs)
            cx.act(csT, tA, AF.Sin, scale=-1.0, bias=math.pi / 2)
            cx.act(mgT, lrdtT, AF.Exp)
            cx.tt("dve", csT, csT, mgT, ALU.mult)
            cx.tt("dve", snT, snT, mgT, ALU.mult)
            cx.tt("dve", tA, lrT, lrT, ALU.mult)
            cx.tt("dve", tB, liT, liT, ALU.mult)
            cx.tt("dve", tA, tA, tB, ALU.add)
            cx.op("dve", lambda: nc.vector.reciprocal(out=rec.ap, in_=tA.ap), [tA], [rec])
            cx.ts("dve", csT, csT, -1.0, ALU.add)
            cx.tt("dve", tA, csT, lrT, ALU.mult)
            cx.tt("dve", tB, snT, liT, ALU.mult)
            cx.tt("dve", tA, tA, tB, ALU.add)
            cx.tt("dve", coT[:, 0, :], tA, rec, ALU.mult)
            cx.tt("dve", tA, snT, lrT, ALU.mult)
            cx.tt("dve", tB, csT, liT, ALU.mult)
            cx.tt("dve", tA, tA, tB, ALU.subtract)
            cx.tt("dve", coT[:, 1, :], tA, rec, ALU.mult)
            cx.memset("dve", onesf[:, 0:128], 1.0)
            cx.act(small[:, 16:32], small[:, 0:16], AF.Exp)
            for hf in range(2):
                sl = slice(hf * 512, (hf + 1) * 512)
                for ri in range(2):
                    for qq in range(4):
                        q = hf * 4 + qq
                        cx.ts("dve", dg[:, qq * 128:(qq + 1) * 128], identf, coT[:, ri, q:q + 1], ALU.mult)
                        cx.mm(pA[ri][:, qq * 128:(qq + 1) * 128], onesf[:, 0:128], dg[:, qq * 128:(qq + 1) * 128])
                core, coim = pA[0], pA[1]
                if hf == 1:
                    lr, li = dtb, THA[:, :]
                dt8 = small[:, 16 + hf * 8:16 + hf * 8 + 8].bl(64)
                cx.tt("dve", lrdt[:, :].re("p (g e) -> p g e", g=8), lr[:, :].re("p (g e) -> p g e", g=8), dt8, ALU.mult)
                cx.tt("dve", lidt[:, :].re("p (g e) -> p g e", g=8), li[:, :].re("p (g e) -> p g e", g=8), dt8, ALU.mult)
                cx.act(ang[:, :], lidt[:, :], AF.Copy, scale=jcol)
                reduce_angle(ang, tf, 512)
                sincos(ang, sn, cs, tf)
                cx.act(mg[:, :], lrdt[:, :], AF.Exp, scale=jcol)
                cx.tt("dve", tf[:, :], mg[:, :], cs[:, :], ALU.mult)
                for qq in range(4):
                    cx.tr(pm[:, qq * 128:(qq + 1) * 128], tf[:, qq * 128:(qq + 1) * 128], identf)
                cx.cp("act", Fre[:, hf * 4:hf * 4 + 4, :].re("p q j -> p (q j)"), pm[:, :])
                cx.cp("act", f127[:, 0, hf * 4:hf * 4 + 4], pm[:, :].re("p (q j) -> p q j", q=4)[:, :, 127])
                cx.cp("act", a1t[:, 0, hf * 4:hf * 4 + 4], pm[:, :].re("p (q j) -> p q j", q=4)[:, :, 1])
                cx.tt("dve", tf[:, :], mg[:, :], sn[:, :], ALU.mult)
                for qq in range(4):
                    cx.tr(pm[:, qq * 128:(qq + 1) * 128], tf[:, qq * 128:(qq + 1) * 128], identf)
                cx.cp("act", Fim[:, hf * 4:hf * 4 + 4, :].re("p q j -> p (q j)"), pm[:, :])
                cx.cp("act", f127[:, 1, hf * 4:hf * 4 + 4], pm[:, :].re("p (q j) -> p q j", q=4)[:, :, 127])
                cx.cp("act", a1t[:, 1, hf * 4:hf * 4 + 4], pm[:, :].re("p (q j) -> p q j", q=4)[:, :, 1])
                cx.act(mg[:, :], lrdt[:, :], AF.Exp, scale=njcol)
                cx.tt("dve", cs[:, :], cs[:, :], mg[:, :], ALU.mult)
                cx.tt("dve", sn[:, :], sn[:, :], mg[:, :], ALU.mult)
                cx.tt("dve", tf[:, :], core[:, :], cs[:, :], ALU.mult)
                cx.tt("dve", mg[:, :], coim[:, :], sn[:, :], ALU.mult)
                cx.tt("dve", Ere[:, sl], tf[:, :], mg[:, :], ALU.add)
                cx.tt("dve", tf[:, :], coim[:, :], cs[:, :], ALU.mult)
                cx.tt("dve", mg[:, :], core[:, :], sn[:, :], ALU.mult)
                cx.tt("dve", Eim[:, sl], tf[:, :], mg[:, :], ALU.subtract)
            sm8 = small[:, 0:8]
            cx.tt("dve", A128[:, 0, :], a1t[:, 0, :], f127[:, 0, :], ALU.mult)
            cx.tt("dve", sm8, a1t[:, 1, :], f127[:, 1, :], ALU.mult)
            cx.tt("dve", A128[:, 0, :], A128[:, 0, :], sm8, ALU.subtract)
            cx.tt("dve", A128[:, 1, :], a1t[:, 0, :], f127[:, 1, :], ALU.mult)
            cx.tt("dve", sm8, a1t[:, 1, :], f127[:, 0, :], ALU.mult)
            cx.tt("dve", A128[:, 1, :], A128[:, 1, :], sm8, ALU.add)

        def rms_stats(P, xsrc, junk, small):
            if RMS1:
                cx.act(junk[0:P, 0:1024], xsrc[0:P, 0:1024], AF.Square)
                cx.red(small[0:P, 2:3], junk[0:P, 0:1024])
            else:
                cx.act(junk[0:P, 0:512], xsrc[0:P, 0:512], AF.Square)
                cx.red(small[0:P, 0:1], junk[0:P, 0:512])
                cx.act(junk[0:P, 0:512], xsrc[0:P, 512:1024], AF.Square)
                cx.red(small[0:P, 1:2], junk[0:P, 0:512])
                cx.tt("dve", small[0:P, 2:3], small[0:P, 0:1], small[0:P, 1:2], ALU.add)
            cx.ts("dve", small[0:P, 2:3], small[0:P, 2:3], 1.0 / D, ALU.mult, EPS, ALU.add)
            if POW:
                cx.tt("pool", small[0:P, 4:5], small[0:P, 2:3], nhalf[0:P, 0:1], ALU.pow)
            else:
                cx.act(small[0:P, 3:4], small[0:P, 2:3], AF.Ln)
                cx.act(small[0:P, 4:5], small[0:P, 3:4], AF.Exp, scale=-0.5)

        def rms_scale(P, xsrc, out_bf):
            rms_stats(P, xsrc, out_bf if RMS1 else THA, smallA)
            if ENG_XS == "act":
                cx.act(out_bf[0:P, :], xsrc[0:P, :], AF.Copy, scale=smallA[0:P, 4:5])
            else:
                cx.ts("dve", out_bf[0:P, :], xsrc[0:P, :], smallA[0:P, 4:5], ALU.mult)

        def transposes_to(P, src, dstT, bank):
            pb = bfv(bank)
            for kc in range(8):
                cx.tr(pb[:, kc * 128:kc * 128 + P], src[0:P, kc * 128:(kc + 1) * 128], ident[0:P, 0:P])
            if P == 128:
                cx.cp("act", dstT[:, :, :].re("p k t -> p (k t)"), pb[:, :])
            else:
                cx.cp("act", dstT[:, :, 0:P], pb[:, :].re("p (k t) -> p k t", k=8)[:, :, 0:P])

        YG = 3

        def inproj(P, c0, n, pst):
            for kc in range(8):
                cx.mm(pst[0:P, 0:n], hT[:, kc, 0:P], wink[kc][:, c0:c0 + n], start=(kc == 0), stop=(kc == 7))
                if kc % YG == YG - 1:
                    yield

        def silu2(P, pst, n, dst):
            cx.act(THA[0:P, 0:n], pst[0:P, 0:n], AF.Tanh, scale=0.5)
            cx.stt(dst[0:P, 0:n], THA[0:P, 0:n], 1.0, pst[0:P, 0:n], ALU.add, ALU.mult)

        cur_layer = [0]

        def inproj_all(P, a):
            p0, p1 = pA
            yield from inproj(P, O_QA, 384, p0)
            silu2(P, p0, 384, a.Q2)
            yield
            yield from inproj(P, O_FA, 384, p1)
            cx.act(THA[0:P, 0:384], p1[0:P, 0:384], AF.Tanh, scale=0.5)
            if FF1 and cur_layer[0] == 0:
                cx.ts("dve", a.FF[0:P, 0:384], THA[0:P, 0:384], 0.5, ALU.mult, 0.5, ALU.add)
            else:
                cx.tt("dve", a.FF[0:P, 0:384], THA[0:P, 0:384], c1r[0:P, :], ALU.mult)
                cx.tt("dve", a.FF[0:P, 0:384], a.FF[0:P, 0:384], c0r[0:P, :], ALU.add)
            cx.act(a.LOGF[0:P, 0:384], a.FF[0:P, 0:384], AF.Ln)
            yield
            yield from inproj(P, O_IA, 384, p0)
            cx.cp("act", a.va[0:P, :], p0[0:P, 0:384])
            yield
            yield from inproj(P, O_ZA, 384, p1)
            silu2(P, p1, 384, a.GA)
            cx.tt("pool", a.GA[0:P, 0:384], a.GA[0:P, 0:384], gAr[0:P, :], ALU.mult)
            yield
            yield from inproj(P, O_QB, 384, p0)
            cx.cp("act", a.QKB[0:P, 0:384], p0[0:P, 0:384])
            yield
            yield from inproj(P, O_VB, 384, p1)
            cx.cp("act", a.vb[0:P, :], p1[0:P, 0:384])
            yield
            yield from inproj(P, O_ZB, 384, p0)
            silu2(P, p0, 384, a.GB)
            cx.tt("pool", a.GB[0:P, 0:384], a.GB[0:P, 0:384], gBr[0:P, :], ALU.mult)
            yield
            yield from inproj(P, O_ZC, 256, p1)
            silu2(P, p1, 256, a.GZC)
            yield
            for kc in range(8):
                cx.mm(p0[0:16, 0:P], wink[kc][:, O_RB:O_RB + 16], hT[:, kc, 0:P], start=(kc == 0), stop=(kc == 7))
            cx.cp("act", rbT[:, 0:P], p0[0:16, 0:P])
            for c in range(2):
                for kc in range(8):
                    cx.mm(p1[:, c * P:(c + 1) * P], wink[kc][:, O_UC + c * 128:O_UC + (c + 1) * 128], hT[:, kc, 0:P],
                          start=(kc == 0), stop=(kc == 7))
                    if kc % 4 == 3:
                        yield
            cx.cp("act", a.uT[:, 0:2 * P], p1[:, 0:2 * P])
            yield
            cx.mm(p0[0:P, 128:320], rbT[:, 0:P], w2b[:, :], start=True, stop=False)
            cx.mm(p0[0:P, 128:320], ones_row[:, 0:P], b2b[:, :], start=False, stop=True)
            cx.act(THA[0:P, 0:192], p0[0:P, 128:320], AF.Exp, scale=-1.0)
            cx.act(a.LG[0:P, :], THA[0:P, 0:192], AF.Ln, bias=1.0)
            yield

        def headnorm(P, pso, G, mixdst, b):
            if HN2:
                cx.act(b.TH[0:P, 0:384], pso[0:P, 0:384], AF.Square)
                o_, p_, g_ = b.OSB[0:P, 0:384], pso[0:P, 0:384], G[0:P, 0:384]
                cx.op("dve", lambda: nc.vector.tensor_tensor(out=o_.ap, in0=p_.ap, in1=g_.ap, op=ALU.mult),
                      [p_, g_, b.TH[0:P, 0:384]], [o_])
            else:
                cx.cp("act", b.OSB[0:P, 0:384], pso[0:P, 0:384])
                cx.act(b.TH[0:P, 0:384], b.OSB[0:P, 0:384], AF.Square)
            cx.red(b.small[0:P, 8:14], b.TH[0:P, 0:384].re("p (h v) -> p h v", h=6))
            cx.ts("dve", b.small[0:P, 8:14], b.small[0:P, 8:14], 1.0 / 64, ALU.mult, EPS, ALU.add)
            if POW:
                cx.tt("pool", b.small[0:P, 20:26], b.small[0:P, 8:14], nhalf[0:P, 0:6], ALU.pow)
            else:
                cx.act(b.small[0:P, 14:20], b.small[0:P, 8:14], AF.Ln)
                cx.act(b.small[0:P, 20:26], b.small[0:P, 14:20], AF.Exp, scale=-0.5)
            if HN2:
                cx.tt("dve", mixdst.re("p (h v) -> p h v", h=6), b.OSB[0:P, 0:384].re("p (h v) -> p h v", h=6),
                      b.small[0:P, 20:26].bl(64), ALU.mult)
            else:
                cx.tt("dve", b.OSB[0:P, 0:384].re("p (h v) -> p h v", h=6), b.OSB[0:P, 0:384].re("p (h v) -> p h v", h=6),
                      b.small[0:P, 20:26].bl(64), ALU.mult)
                cx.tt("dve", mixdst, b.OSB[0:P, 0:384], G[0:P, 0:384], ALU.mult)

        def recur(K, b, qsrc, ksrc_fn, lgsrc, escale, e1bias, vbf, G, mixdst, first, dlast, state_out):
            HK = 6 * K
            p0, p1 = b.p
            if first:
                cx.memset("dve", b.S[:, :], 0.0)
                cx.memset("dve", b.Sbf[0][:, :], 0.0)
            if HILO:
                hi, lo = b.kt, b.kh
                cx.cp("act", hi[:, 0:HK], lgsrc[:, 0:HK])
                cx.tt("dve", lo[:, 0:HK], lgsrc[:, 0:HK], hi[:, 0:HK], ALU.subtract)
                cx.mm(p0[:, 0:HK], ublk_b, hi[:, 0:HK], start=True, stop=False)
                cx.mm(p0[:, 0:HK], ublk_b, lo[:, 0:HK], start=False, stop=True)
                for h in range(6):
                    cx.mm(p0[0:K, 400 + 2 * h:402 + 2 * h], hi[:, h * K:(h + 1) * K], ind_b, start=True, stop=False)
                    cx.mm(p0[0:K, 400 + 2 * h:402 + 2 * h], lo[:, h * K:(h + 1) * K], ind_b, start=False, stop=True)
            else:
                cx.mm(p0[:, 0:HK], ublk, lgsrc[:, 0:HK])
                for h in range(6):
                    cx.mm(p0[0:K, 400 + 2 * h:402 + 2 * h], lgsrc[:, h * K:(h + 1) * K], ind)
                cx.mm(p1[:, 0:HK], lblk, lgsrc[:, 0:HK])
            yield
            cx.act(b.E1[:, 0:HK], p0[:, 0:HK], AF.Exp, scale=escale, bias=e1bias)
            cx.act(b.E2[:, 0:HK], p0[:, 0:HK], AF.Exp, scale=-escale)
            cx.act(b.Dd[0:K, :], p0[0:K, 400:412], AF.Exp, scale=escale)
            yield
            ksrc = ksrc_fn()
            cx.tt("dve", b.qt[:, 0:HK], qsrc[:, 0:HK], b.E1[:, 0:HK], ALU.mult)
            yield
            cx.tt(ENG_KT if K == 32 else "dve", b.kt[:, 0:HK], ksrc[:, 0:HK], b.E2[:, 0:HK], ALU.mult)
            yield
            pb0, pb1 = bfv(p0), bfv(p1)
            for h in range(6):
                cx.tr(pb0[0:K, h * 128:(h + 1) * 128], b.qt[:, h * K:(h + 1) * K], ident)
            cx.cp("act", b.qTs[0:K, :], pb0[0:K, 0:768])
            yield
            for h in range(6):
                cx.tr(pb1[0:K, h * 128:(h + 1) * 128], b.kt[:, h * K:(h + 1) * K], ident)
            cx.cp(ENG_KTS, b.kTs[0:K, :], pb1[0:K, 0:768])
            yield
            for h in range(4):
                cx.mm(p0[:, h * 128:(h + 1) * 128], b.kTs[0:K, h * 128:(h + 1) * 128], b.qTs[0:K, h * 128:(h + 1) * 128])
            yield
            cx.tt("dve", b.attS[:, 0:512].re("p (h t) -> p h t", h=4), p0[:, :].re("p (h t) -> p h t", h=4),
                  ublk.bm(4), ALU.mult)
            yield
            for h in range(4, 6):
                cx.mm(p1[:, (h - 4) * 128:(h - 3) * 128], b.kTs[0:K, h * 128:(h + 1) * 128],
                      b.qTs[0:K, h * 128:(h + 1) * 128])
            yield
            cx.tt("dve", b.attS[:, 512:768].re("p (h t) -> p h t", h=2), p1[:, 0:256].re("p (h t) -> p h t", h=2),
                  ublk.bm(2), ALU.mult)
            yield
            for c in range(2):
                pst = (p0, p1)[c]
                for h in range(6):
                    cx.mm(pst[0:K, h * 64:(h + 1) * 64], b.kt[c * 64:(c + 1) * 64, h * K:(h + 1) * K],
                          vbf[c * 64:(c + 1) * 64, h * 64:(h + 1) * 64])
                yield
                dsl = b.Dd[0:K, :].re("p (h c) -> p h c", c=2)[:, :, c].bl(64)
                cx.tt("dve", b.S[0:K, :], b.S[0:K, :], pst[0:K, 0:384], ALU.add)
                cx.tt("dve", b.S[0:K, :].re("p (h v) -> p h v", h=6), b.S[0:K, :].re("p (h v) -> p h v", h=6), dsl, ALU.mult)
                if c == 0:
                    cx.cp("act", b.Sbf[1][0:K, :], b.S[0:K, :])
                yield
            for h in range(6):
                osl = slice(h * 64, (h + 1) * 64)
                cx.mm(p0[:, osl], b.attS[:, h * 128:(h + 1) * 128], vbf[:, osl], start=True, stop=False)
                cx.mm(p0[0:64, osl], b.qTs[0:K, h * 128:h * 128 + 64], b.Sbf[0][0:K, osl], start=False, stop=True)
                cx.mm(p0[64:128, osl], b.qTs[0:K, h * 128 + 64:h * 128 + 128], b.Sbf[1][0:K, osl], start=False, stop=True)
            yield
            if dlast:
                cx.dma("sync", state_out, b.S[0:K, :].re("p (h v) -> p h v", h=6))
            else:
                cx.cp("act", b.Sbf[0][0:K, :], b.S[0:K, :])
            headnorm(128, p0, G, mixdst, b)
            yield

        def s5_y(P, hf, a, xr_fn, xi_fn):
            pst = SB.p[0]
            for qq in range(4):
                q = hf * 4 + qq
                ysl = slice(qq * 32, (qq + 1) * 32)
                cx.mm(pst[0:P, ysl], xr_fn(q), Cblk[:, 0, q, :], start=True, stop=False)
                cx.mm(pst[0:P, ysl], xi_fn(q), Cblk[:, 1, q, :], start=False, stop=False)
                cx.mm(pst[0:P, ysl], a.uT[:, hf * P:(hf + 1) * P], Ddiag[:, hf * 128 + qq * 32:hf * 128 + (qq + 1) * 32],
                      start=False, stop=True)
            cx.cp("act", SB.ysb[0:P, hf * 128:(hf + 1) * 128], pst[0:P, 0:128])

        def s5_tail(P, a, mixdst):
            y2, inn, th, ge2 = SB.t
            tg = SB.TH
            ysb = SB.ysb
            cx.act(y2[0:P, 0:256], ysb[0:P, :], AF.Square)
            cx.ts("dve", inn[0:P, 0:256], y2[0:P, 0:256], 0.044715, ALU.mult, 1.0, ALU.add)
            cx.tt("dve", inn[0:P, 0:256], inn[0:P, 0:256], ysb[0:P, :], ALU.mult)
            yield
            cx.act(th[0:P, 0:256], inn[0:P, 0:256], AF.Tanh, scale=GC)
            cx.stt(ge2[0:P, 0:256], th[0:P, 0:256], 1.0, ysb[0:P, :], ALU.add, ALU.mult)
            if ENG_S3 == "act":
                cx.act(SB.ycb[0:P, :], ge2[0:P, 0:256], AF.Copy, scale=0.5)
            else:
                cx.ts(ENG_S3, SB.ycb[0:P, :], ge2[0:P, 0:256], 0.5, ALU.mult, 1.0, ALU.mult)
            yield
            pb = bfv(SB.p[1])
            for kc in range(2):
                cx.tr(pb[:, kc * P:(kc + 1) * P], SB.ycb[0:P, kc * 128:(kc + 1) * 128], ident[0:P, 0:P])
            cx.cp("act", SB.ycT[:, 0:2 * P], pb[:, 0:2 * P])
            yield
            pg = SB.p[0]
            for kc in range(2):
                cx.mm(pg[0:P, 0:256], SB.ycT[:, kc * P:(kc + 1) * P], wgl[:, kc, :], start=(kc == 0), stop=False)
            cx.mm(pg[0:P, 0:256], ones_row[:, 0:P], bgl[:, :], start=False, stop=True)
            cx.act(tg[0:P, 0:256], pg[0:P, 0:256], AF.Tanh, scale=0.5)
            yield
            cx.stt(tg[0:P, 0:256], tg[0:P, 0:256], 1.0, ge2[0:P, 0:256], ALU.add, ALU.mult)
            cx.stt(mixdst, tg[0:P, 0:256], 0.125, a.GZC[0:P, 0:256], ALU.mult, ALU.mult)
            yield

        def s5_thread(l, a, mixdst, first, dlast):
            if first:
                cx.memset("dve", SB.cre[:, :], 0.0)
                cx.memset("dve", SB.cim[:, :], 0.0)
            t1, t2, t3, t4 = SB.tb
            f2 = "p q j -> p (q j)"
            for hf in range(2):
                sl = slice(hf * 512, (hf + 1) * 512)
                cx.mm(SB.p[0][:, :], a.uT[:, hf * 128:(hf + 1) * 128], BD[:, hf, 0, :])
                cx.mm(SB.p[1][:, :], a.uT[:, hf * 128:(hf + 1) * 128], BD[:, hf, 1, :])
                yield
                cx.tt("dve", t1[:, :], SB.p[0][:, :], Ere[:, sl], ALU.mult)
                yield
                cx.tt("dve", t2[:, :], SB.p[1][:, :], Eim[:, sl], ALU.mult)
                yield
                cx.tt("dve", t3[:, :], SB.p[0][:, :], Eim[:, sl], ALU.mult)
                yield
                cx.tt("dve", t4[:, :], SB.p[1][:, :], Ere[:, sl], ALU.mult)
                yield
                cx.tt("dve", SB.wre[:, :], t1[:, :], t2[:, :], ALU.subtract)
                yield
                cx.tt(ENG_S1, SB.wim[:, :], t3[:, :], t4[:, :], ALU.add)
                yield
                for k2 in range(2):
                    pst = SB.p[k2]
                    for ri, wsrc in enumerate((SB.wre, SB.wim)):
                        for qq in range(2):
                            ql = 2 * k2 + qq
                            cx.mm(pst[:, (ri * 2 + qq) * 128:(ri * 2 + qq + 1) * 128], wsrc[:, ql * 128:(ql + 1) * 128], u128)
                yield
                for k2 in range(2):
                    pst = SB.p[k2]
                    k = hf * 2 + k2
                    pre_ = pst[:, 0:256].re("p (q j) -> p q j", q=2)
                    pim_ = pst[:, 256:512].re("p (q j) -> p q j", q=2)
                    cx.tt("dve", SB.zcre[:, 2 * k2:2 * k2 + 2, :], pre_, SB.cre[:, 2 * k:2 * k + 2].bl(128), ALU.add)
                    cx.tt("dve", SB.zcim[:, 2 * k2:2 * k2 + 2, :], pim_, SB.cim[:, 2 * k:2 * k + 2].bl(128), ALU.add)
                    cx.tt("dve", SB.z127[:, 0, 2 * k2:2 * k2 + 2], pre_[:, :, 127], SB.cre[:, 2 * k:2 * k + 2], ALU.add)
                    cx.tt("dve", SB.z127[:, 1, 2 * k2:2 * k2 + 2], pim_[:, :, 127], SB.cim[:, 2 * k:2 * k + 2], ALU.add)
                    yield
                m1, m2, m3, m4 = SB.tb
                fsl = slice(hf * 4, hf * 4 + 4)
                cx.tt("dve", m1[:, :], Fre[:, fsl, :].re(f2), SB.zcre[:, :, :].re(f2), ALU.mult)
                yield
                cx.tt(ENG_S1, m2[:, :], Fim[:, fsl, :].re(f2), SB.zcim[:, :, :].re(f2), ALU.mult)
                yield
                cx.tt("dve", m3[:, :], Fre[:, fsl, :].re(f2), SB.zcim[:, :, :].re(f2), ALU.mult)
                yield
                cx.tt("dve", m4[:, :], Fim[:, fsl, :].re(f2), SB.zcre[:, :, :].re(f2), ALU.mult)
                yield
                cx.tt("dve", SB.xreb[:, :, :].re(f2), m1[:, :], m2[:, :], ALU.subtract)
                cx.tt("dve", SB.nximb[:, :, :].re(f2), m3[:, :], m4[:, :], ALU.add)
                yield
                zr, zi = SB.z127[:, 0, :], SB.z127[:, 1, :]
                sm = SB.small[:, 0:4]
                if dlast:
                    Fr, Fi = f127[:, 0, fsl], f127[:, 1, fsl]
                    Xr, Xi = SB.xl[:, 0, :], SB.xl[:, 1, :]
                    cx.tt("dve", Xr, Fr, zr, ALU.mult)
                    cx.tt("dve", sm, Fi, zi, ALU.mult)
                    cx.tt("dve", Xr, Xr, sm, ALU.subtract)
                    cx.tt("dve", Xi, Fr, zi, ALU.mult)
                    cx.tt("dve", sm, Fi, zr, ALU.mult)
                    cx.tt("dve", Xi, Xi, sm, ALU.add)
                    for g2 in range(2):
                        cx.dma("sync", V(rep.t[l, hf * 512:(hf + 1) * 512].rearrange("(q g p) -> g p q", q=4, g=2)[g2], rep),
                               SB.xl[g2 * 64:(g2 + 1) * 64, 0, :], allow_slow_non_contiguous=True)
                        cx.dma("sync", V(imp.t[l, hf * 512:(hf + 1) * 512].rearrange("(q g p) -> g p q", q=4, g=2)[g2], imp),
                               SB.xl[g2 * 64:(g2 + 1) * 64, 1, :], allow_slow_non_contiguous=True)
                else:
                    Ar, Ai = A128[:, 0, fsl], A128[:, 1, fsl]
                    cx.tt(ENG_S2, sm, Ar, zr, ALU.mult)
                    cx.tt(ENG_S2, SB.cre[:, fsl], Ai, zi, ALU.mult)
                    cx.tt(ENG_S2, SB.cre[:, fsl], sm, SB.cre[:, fsl], ALU.subtract)
                    cx.tt(ENG_S2, sm, Ar, zi, ALU.mult)
                    cx.tt(ENG_S2, SB.cim[:, fsl], Ai, zr, ALU.mult)
                    cx.tt(ENG_S2, SB.cim[:, fsl], SB.cim[:, fsl], sm, ALU.add)
                yield
                s5_y(128, hf, a, lambda q: SB.xreb[:, q % 4, :], lambda q: SB.nximb[:, q % 4, :])
                yield
            yield from s5_tail(128, a, mixdst)

        def outproj_residual(P, xres, m):
            transposes_to(P, m, mixT, pA[0])
            yield
            for n, pst in enumerate(pA):
                for kc in range(8):
                    cx.mm(pst[0:P, :], mixT[:, kc, 0:P], woutk[kc][:, n * 512:(n + 1) * 512], start=(kc == 0), stop=(kc == 7))
                    if kc % YG == YG - 1:
                        yield
                cx.tt("dve", xres[0:P, n * 512:(n + 1) * 512], xres[0:P, n * 512:(n + 1) * 512], pst[0:P, :], ALU.add)
                yield

        def final_norm(P, xres, dst, junk):
            rms_stats(P, xres, junk if RMS1 else THA, smallO)
            if RMS1:
                cx.stt(xres[0:P, :], xres[0:P, :], smallO[0:P, 4:5], gfr[0:P, :], ALU.mult, ALU.mult)
            else:
                for n in range(2):
                    sl = slice(n * 512, (n + 1) * 512)
                    cx.stt(xres[0:P, sl], xres[0:P, sl], smallO[0:P, 4:5], gfr[0:P, sl], ALU.mult, ALU.mult)
            cx.dma("sync", dst, xres[0:P, :])

        def thread_A(l, i):
            a = sets[i % 2]
            xb = xin[i % 3]
            src = xp if l == 0 else y0
            cx.dma("sync", xb[:, :], src[i * 128:(i + 1) * 128, :])
            rms_scale(128, xb, xsb)
            yield
            transposes_to(128, xsb, hT, pA[0])
            yield
            yield from inproj_all(128, a)

        def thread_O(l, i):
            xb = xin[i % 3]
            yield from outproj_residual(128, xb, mixb[i % 2])
            if l == nl - 1:
                final_norm(128, xb, yp[i * 128:(i + 1) * 128, :], mixb[i % 2])
            else:
                cx.dma("sync", y0[i * 128:(i + 1) * 128, :], xb[:, :])
            yield

        def thread_AO(l, g):
            if g >= 1:
                yield from thread_O(l, g - 1)
            if g + 1 < nt:
                yield from thread_A(l, g + 1)

        def thread_H(l, i):
            a = sets[i % 2]

            def ka_fn():
                cx.ts("pool", a.FF[:, 0:384], a.FF[:, 0:384], -1.0, ALU.mult, 1.0, ALU.add)
                return a.FF
            yield from recur(64, HB, a.Q2, ka_fn, a.LOGF, 1.0, LN_HALF, a.va, a.GA, mixb[i % 2][:, 0:384],
                             i == 0, i == nt - 1, V(hgp.t[l].rearrange("h k v -> k h v"), hgp))

        def thread_G(l, i):
            a = sets[i % 2]
            yield from recur(32, GBs, a.QKB, lambda: a.QKB[:, 192:384], a.LG, -1.0 / 16, LN_QS, a.vb, a.GB,
                             mixb[i % 2][:, 384:768], i == 0, i == nt - 1, V(glp.t[l].rearrange("h k v -> k h v"), glp))

        def thread_S(l, i):
            a = sets[i % 2]
            yield from s5_thread(l, a, mixb[i % 2][:, 768:1024], i == 0, i == nt - 1)

        def run_threads(gens, weights=None):
            active = list(gens)
            wts = dict(zip(map(id, gens), weights or [1] * len(gens)))
            while active:
                for g in list(active):
                    try:
                        for _ in range(wts[id(g)]):
                            next(g)
                    except StopIteration:
                        active.remove(g)

        class FV:
            def __init__(self, buf):
                self.v = fview(buf)

            def __getitem__(self, idx):
                return self.v[idx]
        xsb_f32, hT_f32, mixT_f32 = FV(xsb), FV(hT), FV(mixT)

        def sample_layer(l):
            P = 16
            a = sets[0]
            m = mixb[0]
            S0a, S1a = xin[1], xin[0]
            bh = V(SB.t[0].t[0:96, :].rearrange("p (a b) -> p a b", a=8), SB.t[0])
            obh = SB.t[1][0:96, 0:64]
            obh2 = SB.t[1][0:96, 64:128]
            if l == 0:
                cx.dma("sync", xsr[:, :], xsm[:, :])
            rms_scale(P, xsr, xsb)
            transposes_to(P, xsb, hT, pA[0])
            run(inproj_all(P, a))
            E1, E2, E3, OSB, TH = HB.E1, HB.E2, HB.E3, HB.OSB, HB.TH
            cx.ts("dve", E1[0:P, 0:384], a.FF[0:P, 0:384], -1.0, ALU.mult, 1.0, ALU.add)
            cx.ts("dve", E2[0:P, 0:384], a.Q2[0:P, 0:384], 0.5, ALU.mult)
            cx.cp("dve", E3[0:P, 0:384], a.va[0:P, :])
            cx.act(OSB[0:P, 0:192], a.LG[0:P, :], AF.Exp, scale=-1.0 / 16)
            cx.ts("dve", TH[0:P, 0:192], a.QKB[0:P, 0:192], 32 ** -0.5, ALU.mult)
            cx.cp("dve", a.LOGF[0:P, 0:384], a.vb[0:P, :])
            segs = [[a.FF[0:P, 0:384], E1[0:P, 0:384], E2[0:P, 0:384], E3[0:P, 0:384]],
                    [OSB[0:P, 0:192], a.QKB[0:P, 192:384], TH[0:P, 0:192], a.LOGF[0:P, 0:384]]]
            for mi in range(2):
                for j in range(4):
                    cx.dma("gq", scr_s[mi][j][:, :], segs[mi][j])
            for mi, (K, st_in, st_out, G, mixcols) in enumerate((
                    (64, st_hg, hgs, a.GA, slice(0, 384)),
                    (32, st_gl, gls, a.GB, slice(384, 768)))):
                if mi == 1 and SAMPLE_SPLIT:
                    bh = V(SB.t[2].t[0:96, :].rearrange("p (a b) -> p a b", a=8), SB.t[2])
                    obh = SB.t[3][0:96, 0:64]
                    obh2 = SB.t[3][0:96, 64:128]
                    flat = "p a b -> p (a b)"
                    s0bufs = [V(SB.xre.t[:, :, :].rearrange(flat), SB.xre), V(SB.nxim.t[:, :, :].rearrange(flat), SB.nxim)]
                    S1v = fview(mixb[1])
                else:
                    s0bufs = [xin[1][:, :], xin[2][:, :]]
                    S1v = xin[0][:, :]
                for j in range(3):
                    cx.dma("gq", bh[:, j, 0:K], V(scr_s[mi][j].t.rearrange("b (h k) -> (b h) k", h=6), scr_s[mi][j]))
                cx.dma("gq", bh[:, 3, :], V(scr_s[mi][3].t.rearrange("b (h v) -> (b h) v", h=6), scr_s[mi][3]))
                KH = K // 4
                KV = KH * 64
                s3 = "p (k v) -> p k v"
                for half in range(4):
                    ks = slice(half * KH, (half + 1) * KH)
                    a_, kk_, q_, v_ = bh[:, 0, ks], bh[:, 1, ks], bh[:, 2, ks], bh[:, 3, :]
                    S0 = s0bufs[half % 2][0:96, 0:KV]
                    S1 = S1v[0:96, 0:KV]
                    cx.dma("sync", S0, st_in[l, :, half * KV:(half + 1) * KV])
                    cx.tt("dve", S0.re(s3, k=KH), S0.re(s3, k=KH), a_.bl(64), ALU.mult)
                    cx.tt("dve", S1.re(s3, k=KH), kk_.bl(64), v_.bm(KH), ALU.mult)
                    cx.tt("dve", S0, S0, S1, ALU.add)
                    cx.dma("gq", st_out[l, :, half * KV:(half + 1) * KV], S0)
                    cx.tt("dve", S1.re(s3, k=KH), S0.re(s3, k=KH), q_.bl(64), ALU.mult)
                    if half == 0:
                        cx.red(obh, S1.re("p (k v) -> p v k", k=KH))
                    else:
                        cx.red(obh2, S1.re("p (k v) -> p v k", k=KH))
                        cx.tt("dve", obh, obh, obh2, ALU.add)
                cx.tt("dve", bh[:, 4, :], obh, obh, ALU.mult)
                cx.red(bh[:, 5, 0:1], bh[:, 4, :])
                cx.ts("dve", bh[:, 5, 0:1], bh[:, 5, 0:1], 1.0 / 64, ALU.mult, EPS, ALU.add)
                cx.act(bh[:, 5, 1:2], bh[:, 5, 0:1], AF.Ln)
                cx.act(bh[:, 5, 2:3], bh[:, 5, 1:2], AF.Exp, scale=-0.5)
                cx.ts("dve", obh, obh, bh[:, 5, 2:3], ALU.mult)
                cx.dma("gq", scr_o[mi, :, :], obh)
                otok = (E1, E2)[mi]
                cx.dma("gq", otok[0:P, 0:384], V(scr_o.t[mi].rearrange("(b h) v -> b (h v)", h=6), scr_o))
                cx.tt("dve", m[0:P, mixcols], otok[0:P, 0:384], G[0:P, 0:384], ALU.mult)
            s1_ = sets[1]
            if S5S_PRIV:
                x0T = V(s1_.Q2.t[:, 0:256].rearrange("p (r q b) -> p r q b", r=2, q=8), s1_.Q2)
                xsT = V(s1_.FF.t[:, 0:256].rearrange("p (r q b) -> p r q b", r=2, q=8), s1_.FF)
                x0w = [[THA, xsb_f32], [hT_f32, mixT_f32]]
            else:
                x0T = V(SB.xre.t[:, :, :].rearrange("p a b -> p (a b)")[:, 0:256].rearrange("p (r q b) -> p r q b", r=2, q=8), SB.xre)
                xsT = V(SB.xre.t[:, :, :].rearrange("p a b -> p (a b)")[:, 256:512].rearrange("p (r q b) -> p r q b", r=2, q=8), SB.xre)
                x0w = [[SB.t[0], SB.t[1]], [SB.t[2], SB.t[3]]]
            xsTb = V(SB.wre.t[:, 0:256].rearrange("p (r q b) -> p r q b", r=2, q=8), SB.wre)
            pq0, pq1 = SB.p
            for ri, stx in enumerate((st_re, st_im)):
                for hf in range(2):
                    cx.dma("sync", x0w[ri][hf][0:16, :], stx[l, :, hf * 512:(hf + 1) * 512])
            for ri in range(2):
                for q in range(8):
                    cx.tr(pq0[:, (ri * 8 + q) * 16:(ri * 8 + q + 1) * 16],
                          x0w[ri][q // 4][0:16, (q % 4) * 128:(q % 4 + 1) * 128], identf[0:16, 0:16])
            cx.cp("act", x0T.re("p r q b -> p (r q b)"), pq0[:, 0:256])
            for ri in range(2):
                for q in range(8):
                    kc, qq = q // 4, q % 4
                    cx.mm(pq1[:, (ri * 8 + q) * 16:(ri * 8 + q + 1) * 16], BD[:, kc, ri, qq * 128:(qq + 1) * 128],
                          a.uT[:, kc * P:(kc + 1) * P])
            Bu = pq1[:, 0:256].re("p (r q b) -> p r q b", r=2, q=8)
            if S5S_PRIV:
                tq = [V(buf.t[:, 0:128].rearrange("p (q b) -> p q b", q=8), buf)
                      for buf in (s1_.LOGF, s1_.GA, s1_.QKB, s1_.GB)]
            else:
                tq = [V(SB.nxim.t[:, :, :].rearrange("p a b -> p (a b)")[:, o_:o_ + 128].rearrange("p (q b) -> p q b", q=8), SB.nxim)
                      for o_ in (0, 128)] + \
                     [V(buf.t[:, 0:128].rearrange("p (q b) -> p q b", q=8), buf) for buf in (HB.E1, HB.E2)]
            cor, coi = coT[:, 0, :].bl(16), coT[:, 1, :].bl(16)
            ar, ai = a1t[:, 0, :].bl(16), a1t[:, 1, :].bl(16)
            cx.tt("dve", tq[0], Bu[:, 0], cor, ALU.mult)
            cx.tt("dve", tq[1], Bu[:, 1], coi, ALU.mult)
            cx.tt("dve", tq[0], tq[0], tq[1], ALU.subtract)
            cx.tt("dve", tq[1], Bu[:, 0], coi, ALU.mult)
            cx.tt("dve", tq[2], Bu[:, 1], cor, ALU.mult)
            cx.tt("dve", tq[1], tq[1], tq[2], ALU.add)
            cx.tt("dve", tq[2], x0T[:, 0], ar, ALU.mult)
            cx.tt("dve", tq[3], x0T[:, 1], ai, ALU.mult)
            cx.tt("dve", tq[2], tq[2], tq[3], ALU.subtract)
            cx.tt("dve", xsT[:, 0], tq[2], tq[0], ALU.add)
            cx.tt("dve", tq[2], x0T[:, 1], ar, ALU.mult)
            cx.tt("dve", tq[3], x0T[:, 0], ai, ALU.mult)
            cx.tt("dve", tq[2], tq[2], tq[3], ALU.add)
            cx.tt("dve", xsT[:, 1], tq[2], tq[1], ALU.add)
            cx.cp("dve", xsTb[:, 0], xsT[:, 0])
            cx.cp("dve", xsTb[:, 1], xsT[:, 1])
            for ri, dst in enumerate((res_o, ims_o)):
                for hf in range(2):
                    for qq in range(4):
                        cx.tr(SB.p[ri][0:16, qq * 128:(qq + 1) * 128], xsT[:, ri, hf * 4 + qq, :], identf)
                    cx.cp("act", x0w[ri][hf][0:16, :], SB.p[ri][0:16, :])
                    cx.dma("gq", dst[l, :, hf * 512:(hf + 1) * 512], x0w[ri][hf][0:16, :])
            for hf in range(2):
                s5_y(P, hf, a, lambda q: xsTb[:, 0, q, :], lambda q: xsTb[:, 1, q, :])
            run(s5_tail(P, a, m[0:P, 768:1024]))
            run(outproj_residual(P, xsr, m))
            if l == nl - 1:
                final_norm(P, xsr, ysm[:, :], m)

        if FILL:
            fdst = [fill_bank.t[:, 0:FILL_N], fill_bank.t[:, 512 - FILL_N:512]]
            fid = cb.t[:, 0:128]
            frhs = cb.t[:, 0:FILL_N]

            def filler(k):
                nc.tensor.matmul(fdst[k % 2], lhsT=fid, rhs=frhs, start=True, stop=True)
            cx.filler = filler
            cx.mm(fill_bank[:, 0:128], ident, ident)
        for l in range(nl):
            cur_layer[0] = l
            if SCHED:
                if cx.recording is None:
                    cx.recording = []
                cx.tag = "prep"
                pg = prep_layer(l)
                next(pg)
                cx.tag = "load"
                run(load_weights(l))
                cx.tag = "prep"
                run(pg)
            else:
                run(load_weights(l))
                run(prep_layer(l))
            if not (SAMPLE_LAST and (l < nl - 1 or not LAST_FIRST)):
                cx.tag = "sample"
                sample_layer(l)
            cx.tag = "tiles"
            run(thread_A(l, 0))
            for g in range(nt):
                cx.tag = "g%02d" % g
                run_threads([thread_S(l, g), thread_H(l, g), thread_G(l, g), thread_AO(l, g)], TW)
            run(thread_O(l, nt - 1))
            if SAMPLE_LAST and (l < nl - 1 or not LAST_FIRST):
                cx.tag = "sample"
                sample_layer(l)
            if SCHED and (l == nl - 1 or not ONE_REC):
                cx.schedule_and_emit(SWIN)
                print("layer", l, "sim_time_us", round(cx.sim_time / 1e3, 1), cx.sim_busy, "crit_us", round(cx.crit / 1e3, 1))
                if VERBOSE:
                    print(cx.sim_tags)
        cx.finish()
    return nc


_NC_CACHE = {}


def kernel(**inp):
    return kernel_impl(inp, 16, 2)


def kernel_impl(inp, nt, nl):
    f32 = np.float32
    g = lambda k: np.ascontiguousarray(np.asarray(inp[k], dtype=f32))
    cf, cb = make_consts()
    if (nt, nl) not in _NC_CACHE:
        _NC_CACHE[(nt, nl)] = build(nt, nl)
    nc = _NC_CACHE[(nt, nl)]
    x_prompt, x_sample = g("x_prompt"), g("x_sample")
    shg, sgl, sre, sim = g("state_hgrn"), g("state_gla"), g("state_s5_re"), g("state_s5_im")
    norm_gT = np.ascontiguousarray(g("norm_g").reshape(2, 8, 128).transpose(2, 0, 1).reshape(128, 16))
    shared = {
        "norm_gT": norm_gT, "w_in": g("w_in"), "lbl": g("hg_lb_logits"), "hg_g": g("hg_norm_g"),
        "w2": g("gla_w2"), "b2": g("gla_b2"), "gla_g": g("gla_norm_g"),
        "A_re": g("s5_A_re").reshape(2, 1024), "A_im": g("s5_A_im").reshape(2, 1024),
        "B_re": g("s5_B_re"), "B_im": g("s5_B_im"), "C_re": g("s5_C_re"), "C_im": g("s5_C_im"),
        "Dv": g("s5_D"), "ldt": g("s5_log_dt"), "wglu": g("s5_w_glu"), "bglu": g("s5_b_glu"),
        "w_out": g("w_out"), "gfin": g("final_norm_g"), "cf": cf, "cb": cb,
    }
    in_maps = []
    for c in range(NCORES):
        m = dict(shared)
        sl = slice(16 * c, 16 * (c + 1))
        m["xp"] = np.ascontiguousarray(x_prompt[c, :nt * 128])
        m["xsm"] = np.ascontiguousarray(x_sample[sl, 0, :])
        m["st_hg"] = np.ascontiguousarray(shg[:, sl].reshape(2, 96, 4096))
        m["st_gl"] = np.ascontiguousarray(sgl[:, sl].reshape(2, 96, 2048))
        m["st_re"] = np.ascontiguousarray(sre[:, sl].reshape(2, 16, 1024))
        m["st_im"] = np.ascontiguousarray(sim[:, sl].reshape(2, 16, 1024))
        in_maps.append(m)
    res = run_bass_kernel_spmd(nc, in_maps, core_ids=list(range(NCORES)))
    R = res.results
    for n in DBG_NAMES:
        DBG[n] = np.asarray(R[0]["dbg_" + n])
    y_prompt = np.stack([R[c]["yp"] for c in range(NCORES)], 0).reshape(8, nt * 128, 1024)
    y_sample = np.concatenate([R[c]["ysm"] for c in range(NCORES)], 0).reshape(128, 1, 1024)
    hgp = np.stack([R[c]["hgp"] for c in range(NCORES)], 1)
    glp = np.stack([R[c]["glp"] for c in range(NCORES)], 1)
    rep = np.stack([R[c]["rep"].reshape(2, 16, 64) for c in range(NCORES)], 1)
    imp = np.stack([R[c]["imp"].reshape(2, 16, 64) for c in range(NCORES)], 1)
    hgs = np.concatenate([R[c]["hgs"].reshape(2, 16, 6, 64, 64) for c in range(NCORES)], 1)
    gls = np.concatenate([R[c]["gls"].reshape(2, 16, 6, 32, 64) for c in range(NCORES)], 1)
    res_ = np.concatenate([R[c]["res_o"].reshape(2, 16, 16, 64) for c in range(NCORES)], 1)
    ims_ = np.concatenate([R[c]["ims_o"].reshape(2, 16, 16, 64) for c in range(NCORES)], 1)
    return tuple(np.ascontiguousarray(a, dtype=f32) for a in
                 (y_prompt, y_sample, hgp, glp, rep, imp, hgs, gls, res_, ims_))
```
